# Optimizing a Trainium2 kernel written in Bass

```python
import jax, jax.numpy as jnp
from jax import lax
import numpy as np

D_MODEL = 4096
BATCH = 4
SEQ = 2048
DEPTH = 1

GRID_W = 64
CTX_LEN = 256
HEAD_DIM = 128
N_HEADS_TOTAL = D_MODEL // HEAD_DIM
NA_HEADS = N_HEADS_TOTAL // 2
NA_WIDTH = NA_HEADS * HEAD_DIM
CONV_WIDTH = D_MODEL - NA_WIDTH
MIX_WIDTH = NA_WIDTH + CONV_WIDTH
CONV_K = 3
NA_KH = 8
NA_KW = 16
FFN_HIDDEN = -(-8 * D_MODEL // (3 * 256)) * 256
ROPE_THETA = 10000.0
EPS = 1e-6
N_MOD = 6
Q0 = 0
K0 = NA_WIDTH
V0 = 2 * NA_WIDTH
CIN0 = 3 * NA_WIDTH
GB0 = CIN0 + CONV_WIDTH
GC0 = GB0 + CONV_WIDTH
IN_COLS = GC0 + CONV_WIDTH

kernel_name = "hybrid_na_shortconv_dit_layer"


def rmsnorm(x, g):
    xf = x.astype(jnp.float32)
    y = xf * lax.rsqrt(jnp.mean(xf * xf, axis=-1, keepdims=True) + EPS)
    return (y * g.astype(jnp.float32)).astype(x.dtype)


def modulate(xn, shift, scale):
    return xn * (1 + scale) + shift


def split_heads(t):
    b, s, _ = t.shape
    return t.reshape(b, s, -1, HEAD_DIM).transpose(0, 2, 1, 3)


def merge_heads(t):
    b, h, s, d = t.shape
    return t.transpose(0, 2, 1, 3).reshape(b, s, h * d)


def rope_1d(u, pos):
    half = u.shape[-1] // 2
    freqs = ROPE_THETA ** (-jnp.arange(half, dtype=jnp.float32) / half)
    ang = pos[:, None] * freqs[None, :]
    cos, sin = jnp.cos(ang), jnp.sin(ang)
    uf = u.astype(jnp.float32)
    u1, u2 = uf[..., :half], uf[..., half:]
    return jnp.concatenate([u1 * cos - u2 * sin, u1 * sin + u2 * cos], axis=-1)


def axial_rope(t, row_pos, col_pos):
    d = t.shape[-1] // 2
    out = jnp.concatenate([rope_1d(t[..., :d], row_pos), rope_1d(t[..., d:], col_pos)], axis=-1)
    return out.astype(t.dtype)


def neighbourhood_attention(q, k, v, k_ctx, v_ctx, rpb):
    b, h, s, dh = q.shape
    rows = s // GRID_W
    kh = min(NA_KH, rows)
    kw = NA_KW
    scale = dh ** -0.5
    qg = q.reshape(b, h, rows, GRID_W, dh)
    kg = k.reshape(b, h, rows, GRID_W, dh)
    vg = v.reshape(b, h, rows, GRID_W, dh)
    row_start = jnp.clip(jnp.arange(rows) - kh // 2, 0, rows - kh)
    col = jnp.arange(GRID_W)
    col_start = jnp.clip(col - kw // 2, 0, GRID_W - kw)
    col_mask = (col[None, :] >= col_start[:, None]) & (col[None, :] < col_start[:, None] + kw)
    dc_idx = jnp.clip(col[None, :] - col[:, None] + kw - 1, 0, 2 * kw - 2)
    rpb_c = rpb[:, :, dc_idx]

    def row_block(args):
        r, rs = args
        q_r = lax.dynamic_index_in_dim(qg, r, axis=2, keepdims=False)
        k_r = lax.dynamic_slice_in_dim(kg, rs, kh, axis=2)
        v_r = lax.dynamic_slice_in_dim(vg, rs, kh, axis=2)
        s_lat = jnp.einsum('bhqd,bhrkd->bhqrk', q_r, k_r).astype(jnp.float32) * scale
        dr_idx = rs + jnp.arange(kh) - r + (NA_KH - 1)
        bias = jnp.take(rpb_c, dr_idx, axis=1).transpose(0, 2, 1, 3)
        s_lat = jnp.where(col_mask[None, None, :, None, :], s_lat + bias[None].astype(jnp.float32), -jnp.inf)
        s_ctx = jnp.einsum('bhqd,bhld->bhql', q_r, k_ctx).astype(jnp.float32) * scale
        sc = jnp.concatenate([s_lat.reshape(b, h, GRID_W, kh * GRID_W), s_ctx], axis=-1)
        p = jax.nn.softmax(sc, axis=-1).astype(v.dtype)
        p_lat = p[..., :kh * GRID_W].reshape(b, h, GRID_W, kh, GRID_W)
        return (jnp.einsum('bhqrk,bhrkd->bhqd', p_lat, v_r)
                + jnp.einsum('bhql,bhld->bhqd', p[..., kh * GRID_W:], v_ctx))

    out = lax.map(row_block, (jnp.arange(rows), row_start))
    return out.transpose(1, 2, 0, 3, 4).reshape(b, h, s, dh)


def context_attention(q, k, v):
    sc = jnp.einsum('bhqd,bhkd->bhqk', q, k).astype(jnp.float32) * (q.shape[-1] ** -0.5)
    p = jax.nn.softmax(sc, axis=-1).astype(v.dtype)
    return jnp.einsum('bhqk,bhkd->bhqd', p, v)


def short_conv(u, w):
    up = jnp.pad(u, ((0, 0), (1, 1), (0, 0)))
    return w[0] * up[:, :-2] + w[1] * up[:, 1:-1] + w[2] * up[:, 2:]


def gated_short_conv(u, gate_b, gate_c, w):
    return gate_b * short_conv(gate_c * u, w)


def mix_output(attn_o, conv_o, g_grp, w_out):
    y = jnp.concatenate([rmsnorm(attn_o, g_grp[:NA_WIDTH]), rmsnorm(conv_o, g_grp[NA_WIDTH:])], axis=-1)
    return y @ w_out


def swiglu(hn, w_gate_up, w_down):
    gu = hn @ w_gate_up
    return (jax.nn.silu(gu[..., :FFN_HIDDEN]) * gu[..., FFN_HIDDEN:]) @ w_down


def setup_inputs(seed: int = 0) -> dict:
    key = jax.random.key(seed)
    ks = jax.random.split(key, 16)
    f32 = jnp.float32

    def nrm(k, shape, scale):
        return jax.random.normal(k, shape, f32) * scale

    def gain(k, shape):
        return 1.0 + 0.02 * jax.random.normal(k, shape, f32)

    return {
        "x": nrm(ks[0], (BATCH, SEQ, D_MODEL), 1.0),
        "c": nrm(ks[1], (BATCH, D_MODEL), 1.0),
        "ctx": nrm(ks[2], (BATCH, CTX_LEN, D_MODEL), 1.0),
        "c_ctx": nrm(ks[3], (D_MODEL,), 1.0),
        "ada_w": nrm(ks[4], (DEPTH, D_MODEL, N_MOD * D_MODEL), 0.5 * D_MODEL ** -0.5),
        "ada_b": nrm(ks[5], (DEPTH, N_MOD * D_MODEL), 0.02),
        "norm1_g": gain(ks[6], (DEPTH, D_MODEL)),
        "w_in": nrm(ks[7], (DEPTH, D_MODEL, IN_COLS), D_MODEL ** -0.5),
        "conv_w": nrm(ks[8], (DEPTH, CONV_K, CONV_WIDTH), CONV_K ** -0.5),
        "rpb": nrm(ks[9], (DEPTH, NA_HEADS, 2 * NA_KH - 1, 2 * NA_KW - 1), 0.1),
        "group_norm_g": gain(ks[10], (DEPTH, MIX_WIDTH)),
        "w_out": nrm(ks[11], (DEPTH, MIX_WIDTH, D_MODEL), MIX_WIDTH ** -0.5),
        "norm2_g": gain(ks[12], (DEPTH, D_MODEL)),
        "w_gate_up": nrm(ks[13], (DEPTH, D_MODEL, 2 * FFN_HIDDEN), D_MODEL ** -0.5),
        "w_down": nrm(ks[14], (DEPTH, FFN_HIDDEN, D_MODEL), FFN_HIDDEN ** -0.5),
        "final_norm_g": gain(ks[15], (D_MODEL,)),
    }


def reference(x, c, ctx, c_ctx, ada_w, ada_b, norm1_g, w_in, conv_w, rpb, group_norm_g, w_out,
              norm2_g, w_gate_up, w_down, final_norm_g):
    b, s, _ = x.shape
    t = jnp.arange(s)
    row_pos = (t // GRID_W).astype(jnp.float32)
    col_pos = (t % GRID_W).astype(jnp.float32)
    for l in range(DEPTH):
        update_ctx = l < DEPTH - 1
        mod = (jax.nn.silu(c) @ ada_w[l] + ada_b[l]).reshape(b, N_MOD, D_MODEL)
        mod_c = (jax.nn.silu(c_ctx) @ ada_w[l] + ada_b[l]).reshape(N_MOD, D_MODEL)
        sh1, sc1, g1, sh2, sc2, g2 = [mod[:, i, None, :] for i in range(N_MOD)]
        csh1, csc1, cg1, csh2, csc2, cg2 = [mod_c[i] for i in range(N_MOD)]

        h = modulate(rmsnorm(x, norm1_g[l]), sh1, sc1)
        hc = modulate(rmsnorm(ctx, norm1_g[l]), csh1, csc1)
        p = h @ w_in[l]
        q = axial_rope(split_heads(p[..., Q0:K0]), row_pos, col_pos)
        k = axial_rope(split_heads(p[..., K0:V0]), row_pos, col_pos)
        v = split_heads(p[..., V0:CIN0])
        if update_ctx:
            pc = hc @ w_in[l]
            kv_c = pc[..., K0:CIN0]
        else:
            kv_c = hc @ w_in[l][:, K0:CIN0]
        k_c = split_heads(kv_c[..., :NA_WIDTH])
        v_c = split_heads(kv_c[..., NA_WIDTH:])
        attn_o = merge_heads(neighbourhood_attention(q, k, v, k_c, v_c, rpb[l]))
        conv_o = gated_short_conv(p[..., CIN0:GB0], p[..., GB0:GC0], p[..., GC0:], conv_w[l])
        x = x + g1 * mix_output(attn_o, conv_o, group_norm_g[l], w_out[l])

        if update_ctx:
            q_c = split_heads(pc[..., Q0:K0])
            attn_c = merge_heads(context_attention(q_c, k_c, v_c))
            conv_c = gated_short_conv(pc[..., CIN0:GB0], pc[..., GB0:GC0], pc[..., GC0:], conv_w[l])
            ctx = ctx + cg1 * mix_output(attn_c, conv_c, group_norm_g[l], w_out[l])
            ctx = ctx + cg2 * swiglu(modulate(rmsnorm(ctx, norm2_g[l]), csh2, csc2), w_gate_up[l], w_down[l])

        x = x + g2 * swiglu(modulate(rmsnorm(x, norm2_g[l]), sh2, sc2), w_gate_up[l], w_down[l])
    return rmsnorm(x, final_norm_g)
```

```python
from collections import deque
from contextlib import ExitStack

import numpy as np
import concourse.bass as bass
import concourse.mybir as mybir
from concourse.bass_utils import run_bass_kernel_spmd

F32 = mybir.dt.float32
BF16 = mybir.dt.bfloat16
AF = mybir.ActivationFunctionType
ALU = mybir.AluOpType
AX = mybir.AxisListType

D = 4096
KC = D // 128
S = 2048
GW = 64
ROWS = S // GW
CTX = 256
NH = 16
FFN = 11008
FC = FFN // 128
EPS = 1e-6
TT = 512
KVL = 1024
KVT = KVL + CTX
NEG = -1.0e30
SCALE = 128 ** -0.5
BLK_CHUNKS = [[0, 1, 2, 3, 4, 5], [0, 1, 2, 3, 5], [0, 1, 2, 3, 6], [0, 1, 2, 3, 6, 7]]
NSLOT = 6
N_TILES = 2
QUARTERS = [(0, 22), (22, 44), (44, 65), (65, 86)]
NWB = 4
NXS = 4


class Trk:
    def __init__(self, sems, dma_sems):
        self.sem = sems
        self.cnt = {e: 0 for e in sems}
        self.plan = {e: [] for e in sems}
        self.lastw = {}
        self.reads = {}
        self.waited = {e: {} for e in sems}
        self.dma_sems = dma_sems
        self.dma_cnt = {q: [0] * len(v) for q, v in dma_sems.items()}
        self.dma_rr = {q: 0 for q in dma_sems}
        self.dma_out = {q: [None] * len(v) for q, v in dma_sems.items()}

    @staticmethod
    def _is_psum(r):
        if isinstance(r, tuple):
            return r[0] in ("pA", "pSg")
        return len(r) >= 2 and r[0] == "p" and r[1].isupper()

    def _excl(self, reads, writes):
        reads, writes = list(reads), list(writes)
        mv = [r for r in reads if self._is_psum(r)]
        return [r for r in reads if not self._is_psum(r)], writes + [r for r in mv if r not in writes]

    def _need(self, eng, deps):
        out = []
        for key, val in deps:
            if self.waited[eng].get(key, 0) >= val:
                continue
            self.waited[eng][key] = val
            out.append((key, val))
        return out

    def _deps(self, reads, writes):
        deps = []
        for r in reads:
            if r in self.lastw:
                deps.append(self.lastw[r])
        for w in writes:
            if w in self.lastw:
                deps.append(self.lastw[w])
            deps.extend(self.reads.get(w, []))
        return deps

    def _commit(self, tok, reads, writes):
        for r in reads:
            self.reads.setdefault(r, []).append(tok)
        for w in writes:
            self.lastw[w] = tok
            self.reads[w] = []

    def retire(self, old, new):
        deps = []
        for o in old:
            if o in self.lastw:
                deps.append(self.lastw.pop(o))
            deps.extend(self.reads.pop(o, []))
        for n in new:
            self.reads.setdefault(n, []).extend(deps)

    def op(self, eng, fn, reads=(), writes=()):
        reads, writes = self._excl(reads, writes)
        waits = self._need(eng, self._deps(reads, writes))
        self.cnt[eng] += 1
        tok = (("e", eng), self.cnt[eng])
        self.plan[eng].append((waits, fn, ("e", eng)))
        self._commit(tok, reads, writes)

    def group(self, eng, fns, reads=(), writes=()):
        reads, writes = self._excl(reads, writes)
        waits = self._need(eng, self._deps(reads, writes))
        n = len(fns)
        for i, fn in enumerate(fns):
            self.plan[eng].append((waits if i == 0 else [], fn, ("e", eng) if i == n - 1 else None))
        self.cnt[eng] += 1
        tok = (("e", eng), self.cnt[eng])
        self._commit(tok, reads, writes)

    def dma(self, q, fn, reads=(), writes=()):
        deps = self._deps(reads, writes)
        s = self.dma_rr[q]
        self.dma_rr[q] = (s + 1) % len(self.dma_sems[q])
        if self.dma_out[q][s] is not None:
            deps.append(self.dma_out[q][s])
        waits = self._need(q, deps)
        self.dma_cnt[q][s] += 16
        tok = (("d", q, s), self.dma_cnt[q][s])
        self.dma_out[q][s] = tok
        self.plan[q].append((waits, fn, ("d", q, s)))
        self._commit(tok, reads, writes)
        return tok

    def final_wait(self, eng, toks):
        waits = self._need(eng, list(toks))
        self.plan[eng].append((waits, None, None))

    def _semof(self, key):
        return self.sem[key[1]] if key[0] == "e" else self.dma_sems[key[1]][key[2]]

    def replay(self, eng, e):
        for waits, fn, sig in self.plan[eng]:
            for key, val in waits:
                e.wait_ge(self._semof(key), val)
            if fn is None:
                continue
            ins = fn(e)
            if sig is not None:
                ins.then_inc(self._semof(sig), 16 if sig[0] == "d" else 1)


def _segments(chunks):
    segs = []
    for s, ch in enumerate(chunks):
        k0, p0 = ch * 128, s * 128
        if segs and segs[-1][0] + segs[-1][2] == k0 and segs[-1][1] + segs[-1][2] == p0 and p0 % 512 != 0:
            segs[-1] = (segs[-1][0], segs[-1][1], segs[-1][2] + 128)
        else:
            segs.append((k0, p0, 128))
    p0 = len(chunks) * 128
    for c in range(2):
        k0 = KVL + c * 128
        pp = p0 + c * 128
        if segs[-1][0] + segs[-1][2] == k0 and segs[-1][1] + segs[-1][2] == pp and pp % 512 != 0:
            segs[-1] = (segs[-1][0], segs[-1][1], segs[-1][2] + 128)
        else:
            segs.append((k0, pp, 128))
    return segs


class _Stop(Exception):
    pass


def build_nc(n_tiles=N_TILES, stop=None):
    nc = bass.Bass("TRN2", target_bir_lowering=False)
    dt = nc.dram_tensor
    xkv = dt("xkv", [n_tiles, 128, KC * KVT], F32, kind="ExternalInput").ap()
    cvec = dt("cvec", [128, KC * 2], F32, kind="ExternalInput").ap()
    ada_w = dt("ada_w", [6 * KC, 128, KC * 128], F32, kind="ExternalInput").ap()
    ada_b = dt("ada_b", [128, 6 * KC], F32, kind="ExternalInput").ap()
    gains = dt("gains", [128, 4 * KC], F32, kind="ExternalInput").ap()
    w_in = dt("w_in", [96, 128, KC * 128], F32, kind="ExternalInput").ap()
    conv_w = dt("conv_w", [128, 48], F32, kind="ExternalInput").ap()
    rope = dt("rope", [n_tiles, 128, 2 * KVL], F32, kind="ExternalInput").ap()
    cmask = dt("cmask", [n_tiles, 128, 2], F32, kind="ExternalInput").ap()
    cst = dt("cst", [128, 3 * 128], F32, kind="ExternalInput").ap()
    bias_t = dt("bias_t", [n_tiles * NH * 4, 128, NSLOT * 128], F32, kind="ExternalInput").ap()
    w_out = dt("w_out", [KC, 128, KC * 128], F32, kind="ExternalInput").ap()
    w_gu = dt("w_gu", [2 * FC, 128, KC * 128], F32, kind="ExternalInput").ap()
    w_dn = dt("w_dn", [KC, 128, FC * 128], F32, kind="ExternalInput").ap()
    out = dt("out", [n_tiles, 128, KC * TT], F32, kind="ExternalOutput").ap()

    es = ExitStack()
    sb = lambda n, s, d: es.enter_context(nc.sbuf_tensor(n, s, d))
    ps = lambda n, s, d: es.enter_context(nc.psum_tensor(n, s, d))
    sm = lambda n: es.enter_context(nc.semaphore(n))
    with es:
        ENG = ["pe", "act", "dve", "pool", "sp"]
        sems = {e: sm("s_" + e) for e in ENG}
        dma_sems = {"sp": [sm(f"dsp{i}") for i in range(10)], "pool": [sm(f"dpl{i}") for i in range(10)]}
        T = Trk(sems, dma_sems)

        cstf = sb("cstf", [128, 384], F32)
        identb = sb("identb", [128, 128], BF16)
        onesb = sb("onesb", [128, 128], BF16)
        epsT = sb("epsT", [128, 1], F32)
        cv = sb("cv", [128, KC * 2], F32)
        cvb = sb("cvb", [128, KC * 2], BF16)
        adab = sb("adab", [128, 6 * KC], F32)
        gn = sb("gn", [128, 4 * KC], F32)
        cw = sb("cw", [128, 48], F32)
        cmk = sb("cmk", [128, 2], F32)
        mod = sb("mod", [128, 6 * KC * 2], F32)
        A1 = sb("A1", [128, KC * 2], F32)
        A2 = sb("A2", [128, KC], F32)
        negm = sb("negm", [128, 1], F32)
        m2 = sb("m2", [2, 512], F32)
        rstd = sb("rstd", [128, KVT], F32)
        wbt = sb("wbt", [128, NWB * KC * 128], BF16)
        ar1 = sb("ar1", [128, KC * KVT], BF16)
        ar2 = sb("ar2", [128, KC * TT], BF16)
        ar3 = sb("ar3", [128, 23552], BF16)
        permf = cstf[:, 128:256]

        def wb(s):
            return wbt[:, s * KC * 128:(s + 1) * KC * 128]

        hT = ar1[:, :].rearrange("p (k t) -> p k t", k=KC)
        x1 = ar1[:, 0:KC * TT * 2].bitcast(F32).rearrange("p (k t) -> p k t", k=KC)
        wdt = ar1[:, KC * TT * 2:KC * TT * 2 + 2 * 22 * 128]

        def wd(s):
            return wdt[:, s * 22 * 128:(s + 1) * 22 * 128]

        xs_ = ar2[:, 0:NXS * KVT * 2].bitcast(F32)
        sq_ = ar2[:, NXS * KVT * 2:NXS * KVT * 2 + NXS * KVT]
        xs = lambda s: xs_[:, s * KVT:(s + 1) * KVT]
        sq = lambda s: sq_[:, s * KVT:(s + 1) * KVT]
        oT = ar2[:, :].rearrange("p (k t) -> p k t", k=KC)
        hn = oT

        _o = [0]

        def carve(n_bf16):
            a = ar3[:, _o[0]:_o[0] + n_bf16]
            _o[0] += n_bf16
            return a

        ropeT = carve(2 * KVL * 2).bitcast(F32)
        qT2 = carve(2 * TT)
        kT2 = carve(2 * KVT)
        vT = carve(KVT)
        vtok2 = carve(2 * 10 * 128)
        stg_ = carve(2 * 512 * 2).bitcast(F32)
        t1 = carve(512 * 2).bitcast(F32)
        t2 = carve(512 * 2).bitcast(F32)
        bt_ = carve(2 * 768 * 2).bitcast(F32)
        sc = carve(1024 * 2).bitcast(F32)
        pb = carve(1024)
        ptb = carve(1024)
        rinv = carve(128 * 2).bitcast(F32)
        att_end = _o[0]
        qT = lambda par: qT2[:, par * TT:(par + 1) * TT]
        kT = lambda par: kT2[:, par * KVT:(par + 1) * KVT]
        vtok = lambda par: vtok2[:, par * 1280:(par + 1) * 1280]
        assert att_end <= 23552, att_end
        stg = lambda s: stg_[:, s * 512:(s + 1) * 512]
        bt = lambda s: bt_[:, s * 768:(s + 1) * 768]
        _o[0] = 2 * KVL * 2
        uS = carve(516 * 2).bitcast(F32)
        cup = carve(516 * 2).bitcast(F32)
        tcv = carve(512 * 2).bitcast(F32)
        xo_ = carve(2 * 512 * 2).bitcast(F32)
        xo = lambda s: xo_[:, s * 512:(s + 1) * 512]
        _o[0] = 0
        actT = carve(22 * 512).rearrange("p (k t) -> p k t", k=22)
        gt_ = carve(2 * 512 * 2).bitcast(F32)
        gtt = lambda s: gt_[:, s * 512:(s + 1) * 512]
        tmpf = carve(512 * 2).bitcast(F32)
        ot_ = carve(2 * 512 * 2).bitcast(F32)
        ott = lambda s: ot_[:, s * 512:(s + 1) * 512]
        sq2_ = carve(2 * 512)
        sq2 = lambda s: sq2_[:, s * 512:(s + 1) * 512]
        assert _o[0] <= 23552, _o[0]

        pS = ps("pS", [128, 1024], F32)
        pT = ps("pT", [128, 512], F32)
        pTb = pT[:, :].bitcast(BF16)
        pM = ps("pM", [128, 512], F32)
        pA = [ps("pA0", [128, 512], F32), ps("pA1", [128, 512], F32)]
        pO = ps("pO", [128, 512], F32)
        pR = ps("pR", [128, 512], F32)
        pRb = pR[:, :].bitcast(BF16)

        st = {"wb": 0, "pa": 0, "xs": 0}

        def chk(name):
            if stop == name:
                raise _Stop()
        bg = deque()
        att = deque()

        def MOD(i, ch, r=0):
            k = 2 * (i * KC + ch) + r
            return mod[:, k:k + 1]

        def load_piece(src2d, n_el, dst):
            a = 2 if n_el <= 4096 else 4
            return lambda e: e.dma_start(out=dst.rearrange("p (a b) -> p a b", a=a),
                                         in_=src2d.rearrange("p (a b) -> p a b", a=a))

        def next_wb(src2d):
            s = st["wb"] % NWB
            st["wb"] += 1
            T.dma("pool", load_piece(src2d, KC * 128, wb(s)), writes=[("wb", s)])
            return s

        def next_pa():
            b = st["pa"] % 2
            st["pa"] += 1
            return b

        deferred = deque()

        def defer(fn):
            deferred.append(fn)

        def flush_deferred():
            while deferred:
                deferred.popleft()()

        def tick(allow_bg=True):
            if att:
                att.popleft()()
            if allow_bg and bg:
                bg.popleft()()

        def mm_group(psum_ap, pname, slot, rhs_fn, n, extra_reads):
            fns = [(lambda kc: (lambda e: e.matmul(psum_ap, wb(slot)[:, kc * 128:(kc + 1) * 128], rhs_fn(kc),
                                                   start=(kc == 0), stop=(kc == KC - 1))))(kc) for kc in range(KC)]
            T.group("pe", fns, reads=[("wb", slot)] + list(extra_reads), writes=[pname])
            flush_deferred()

        XS_REG = [("xs", i) for i in range(NXS)] + [("sq", i) for i in range(NXS)]
        ALL_HT = [("hT", k) for k in range(KC)]
        ALL_OT = [("oT", k) for k in range(KC)]
        ALL_HN = [("hn", k) for k in range(KC)]
        ALL_X1 = [("x1", k) for k in range(KC)]

        T.dma("sp", lambda e: e.dma_start(out=cstf[:, :], in_=cst), writes=["cstf"])
        T.dma("sp", lambda e: e.dma_start(out=cv[:, :], in_=cvec), writes=["cv"])
        T.dma("sp", lambda e: e.dma_start(out=adab[:, :], in_=ada_b), writes=["adab"])
        T.dma("sp", lambda e: e.dma_start(out=gn[:, :], in_=gains), writes=["gn"])
        T.dma("sp", lambda e: e.dma_start(out=cw[:, :], in_=conv_w), writes=["cw"])
        T.op("pool", lambda e: e.memset(epsT[:, :], EPS), writes=["eps"])
        T.op("dve", lambda e: e.tensor_copy(out=identb[:, :], in_=cstf[:, 0:128]), reads=["cstf"], writes=["ident"])
        T.op("dve", lambda e: e.tensor_copy(out=onesb[:, :], in_=cstf[:, 256:384]), reads=["cstf"], writes=["ones"])
        T.op("act", lambda e: e.activation(out=cvb[:, :], in_=cv[:, :], func=AF.Silu), reads=["cv"], writes=["cvb"])

        def ada_piece(pi):
            blk, g = pi // 4, pi % 4

            def f():
                if g == 0:
                    flush_deferred()
                s = next_wb(ada_w[pi])
                fns = [(lambda kci: (lambda e: e.matmul(pM[0:2, :], cvb[:, 2 * (g * 8 + kci):2 * (g * 8 + kci) + 2],
                                                        wb(s)[:, kci * 512:(kci + 1) * 512], start=(g == 0 and kci == 0),
                                                        stop=(g == 3 and kci == 7))))(kci) for kci in range(8)]
                T.group("pe", fns, reads=[("wb", s), "cvb"], writes=["pM"])
                if g == 3:
                    T.op("act", lambda e: e.activation(out=m2[:, :], in_=pM[0:2, :], func=AF.Copy), reads=["pM"], writes=["m2"])

                    def fin():
                        fns2 = [(lambda i: (lambda e: e.transpose(out=pO[:, 256 + 2 * i:258 + 2 * i], in_=m2[:, i * 128:(i + 1) * 128],
                                                                  identity=cstf[0:2, 0:2])))(i) for i in range(4)]
                        T.group("pe", fns2, reads=["m2", "cstf"], writes=["pO"])
                        c0 = blk * 4
                        T.op("dve", lambda e: e.tensor_tensor(
                            out=mod[:, 2 * c0:2 * c0 + 8].rearrange("p (a r) -> p a r", r=2),
                            in0=pO[:, 256:264].rearrange("p (a r) -> p a r", r=2),
                            in1=adab[:, c0:c0 + 4].unsqueeze(2).to_broadcast([128, 4, 2]), op=ALU.add),
                             reads=["pO", "adab"], writes=[("mod", c0 // KC)])
                    defer(fin)
            return f

        pre = deque(ada_piece(pi) for pi in range(2 * KC))
        for pi in range(2 * KC, 6 * KC):
            bg.append(ada_piece(pi))
        bg2 = deque()
        A1v = A1[:, :].rearrange("p (k r) -> p k r", r=2)

        def finish_A1():
            while pre:
                pre.popleft()()
            flush_deferred()
            T.op("dve", lambda e: e.tensor_scalar(out=A1[:, :], in0=mod[:, 2 * KC:4 * KC], scalar1=1.0, scalar2=None,
                                                  op0=ALU.add), reads=[("mod", 1)], writes=["A1"])
            T.op("dve", lambda e: e.tensor_tensor(out=A1v, in0=A1v,
                                                  in1=gn[:, 0:KC].unsqueeze(2).to_broadcast([128, KC, 2]), op=ALU.mult),
                 reads=["A1", "gn"], writes=["A1"])

        def finish_mod():
            while bg:
                bg.popleft()()
            flush_deferred()
            sc2v = mod[:, 2 * 4 * KC:2 * 5 * KC].rearrange("p (k r) -> p k r", r=2)[:, :, 0]
            T.op("dve", lambda e: e.tensor_scalar(out=A2[:, :], in0=sc2v, scalar1=1.0, scalar2=None, op0=ALU.add),
                 reads=[("mod", 4)], writes=["A2"])
            T.op("dve", lambda e: e.tensor_tensor(out=A2[:, :], in0=A2[:, :], in1=gn[:, 2 * KC:3 * KC], op=ALU.mult),
                 reads=["A2", "gn"], writes=["A2"])

        def rms_from_psum(psum_ap, pname, ncols, dst, dname, denom):
            T.op("act", lambda e: e.activation(out=dst, in_=psum_ap, func=AF.Sqrt, bias=epsT[:, 0:1],
                                               scale=1.0 / denom), reads=[pname, "eps"], writes=[dname])
            T.op("dve", lambda e: e.reciprocal(out=dst, in_=dst), reads=[dname], writes=[dname])

        out_toks = []

        early_pass1 = set()

        def begin_pass1(ti):
            T.retire(ALL_HN, XS_REG)
            T.retire([("pSg", 0), ("pSg", 1)], ["pS"])

        def pass1_chunk(ti, kc, early):
            s = st["xs"] % NXS
            st["xs"] += 1
            T.dma("sp", lambda e: e.dma_start(out=xs(s), in_=xkv[ti, :, kc * KVT:(kc + 1) * KVT]), writes=[("xs", s)])
            T.op("act", lambda e: e.activation(out=sq(s), in_=xs(s), func=AF.Square), reads=[("xs", s)], writes=[("sq", s)])
            fns = [
                lambda e: e.matmul(pS[:, 0:512], onesb[:, :], sq(s)[:, 0:512], start=(kc == 0), stop=(kc == KC - 1)),
                lambda e: e.matmul(pS[:, 512:1024], onesb[:, :], sq(s)[:, 512:1024], start=(kc == 0), stop=(kc == KC - 1)),
                lambda e: e.matmul(pT[:, 0:256], onesb[:, :], sq(s)[:, 1024:1280], start=(kc == 0), stop=(kc == KC - 1)),
            ]
            if early:
                defer(lambda: T.group("pe", fns, reads=[("sq", s), "ones"], writes=["pS", "pT"]))
            else:
                T.group("pe", fns, reads=[("sq", s), "ones"], writes=["pS", "pT"])

        def emit_tile(ti):
            first = ti == 0
            AR3_ATT = ["rope", ("qT", 0), ("qT", 1), ("kT", 0), ("kT", 1), "vT", ("vtok", 0), ("vtok", 1), ("stg", 0), ("stg", 1), "t1", "t2", ("bt", 0), ("bt", 1),
                       "sc", "pb", "ptb", "rinv"]
            AR3_CONV = ["uS", "cup", "tcv", ("xo", 0), ("xo", 1)]
            AR3_FFN = [("act", k) for k in range(22)] + [("gt", 0), ("gt", 1), "tmpf", ("ot", 0), ("ot", 1),
                                                         ("sq2", 0), ("sq2", 1)]
            T.retire(ALL_X1, ALL_HT)
            if ti not in early_pass1:
                begin_pass1(ti)
            T.retire(AR3_FFN, AR3_ATT)
            T.retire([("rg", 0), ("rg", 1)], ["rstdA", "rstdB"])

            T.dma("sp", lambda e: e.dma_start(out=ropeT, in_=rope[ti]), writes=["rope"])
            T.dma("sp", lambda e: e.dma_start(out=cmk[:, :], in_=cmask[ti]), writes=["cmk"])

            chk("ada")
            if ti not in early_pass1:
                for kc in range(KC):
                    pass1_chunk(ti, kc, False)
                    for _ in range(2):
                        if pre:
                            pre.popleft()()
            flush_deferred()
            if first:
                finish_A1()
            rms_from_psum(pS[:, 0:1024], "pS", 1024, rstd[:, 0:1024], "rstdA", D)
            rms_from_psum(pT[:, 0:256], "pT", 256, rstd[:, 1024:1280], "rstdB", D)
            for kc in range(KC):
                s = st["xs"] % NXS
                st["xs"] += 1
                T.dma("sp", (lambda kc, s: (lambda e: e.dma_start(out=xs(s), in_=xkv[ti, :, kc * KVT:(kc + 1) * KVT])))(kc, s),
                      writes=[("xs", s)])
                T.op("dve", (lambda s: (lambda e: e.tensor_tensor(out=xs(s), in0=xs(s), in1=rstd[:, :], op=ALU.mult)))(s),
                     reads=[("xs", s), "rstdA", "rstdB"], writes=[("xs", s)])
                T.op("act", (lambda kc, s: (lambda e: e.activation(out=hT[:, kc, 0:KVL], in_=xs(s)[:, 0:KVL], func=AF.Identity,
                                                                  bias=MOD(0, kc, 0), scale=A1[:, 2 * kc:2 * kc + 1])))(kc, s),
                     reads=[("xs", s), "A1", ("mod", 0)], writes=[("hT", kc)])
                T.op("act", (lambda kc, s: (lambda e: e.activation(out=hT[:, kc, KVL:KVT], in_=xs(s)[:, KVL:KVT], func=AF.Identity,
                                                                  bias=MOD(0, kc, 1), scale=A1[:, 2 * kc + 1:2 * kc + 2])))(kc, s),
                     reads=[("xs", s), "A1", ("mod", 0)], writes=[("hT", kc)])
            chk("norm1")
            T.retire(XS_REG, ALL_OT)

            def bias_dma(hd, j):
                u = hd * 4 + j
                s = u % 2
                row = (ti * NH + hd) * 4 + j
                T.dma("sp", lambda e: e.dma_start(out=bt(s), in_=bias_t[row]), writes=[("bt", s)])

            def att_S(hd, j):
                chunks = BLK_CHUNKS[j]
                nlat = len(chunks) * 128
                ncols = nlat + CTX
                u = hd * 4 + j
                s = u % 2
                fns = [(lambda k0, p0, n: (lambda e: e.matmul(pS[:, p0:p0 + n], qT(hd % 2)[:, j * 128:(j + 1) * 128], kT(hd % 2)[:, k0:k0 + n],
                                                              start=True, stop=True)))(k0, p0, n) for (k0, p0, n) in _segments(chunks)]
                T.group("pe", fns, reads=[("qT", hd % 2), ("kT", hd % 2)], writes=["pS"])
                T.op("dve", lambda e: e.scalar_tensor_tensor(out=sc[:, 0:nlat], in0=pS[:, 0:nlat], scalar=SCALE, in1=bt(s)[:, 0:nlat],
                                                             op0=ALU.mult, op1=ALU.add), reads=["pS", ("bt", s)], writes=["sc"])
                T.op("dve", lambda e: e.tensor_scalar(out=sc[:, nlat:ncols], in0=pS[:, nlat:ncols], scalar1=SCALE, scalar2=None,
                                                      op0=ALU.mult), reads=["pS"], writes=["sc"])
                T.op("dve", lambda e: e.tensor_reduce(out=negm[:, 0:1], in_=sc[:, 0:ncols], axis=AX.X, op=ALU.max, negate=True),
                     reads=["sc"], writes=["negm"])
                T.op("act", lambda e: e.activation(out=pb[:, 0:ncols], in_=sc[:, 0:ncols], func=AF.Exp, bias=negm[:, 0:1], scale=1.0),
                     reads=["sc", "negm"], writes=["pb"])
                if j < 3:
                    bias_dma(hd, j + 1)
                elif hd + 1 < NH:
                    bias_dma(hd + 1, 0)

            def att_PT(hd, j):
                ntot = len(BLK_CHUNKS[j]) + 2
                fns = [(lambda sl: (lambda e: e.transpose(out=pTb[:, sl * 128:(sl + 1) * 128], in_=pb[:, sl * 128:(sl + 1) * 128],
                                                          identity=identb[:, :])))(sl) for sl in range(ntot)]
                T.group("pe", fns, reads=["pb", "ident"], writes=["pT"])
                T.op("act", lambda e: e.activation(out=ptb[:, 0:ntot * 128], in_=pTb[:, 0:ntot * 128], func=AF.Copy),
                     reads=["pT"], writes=["ptb"])

            def att_PV(hd, j):
                chunks = BLK_CHUNKS[j] + [8, 9]
                ntot = len(chunks)
                fns = [(lambda sl, ch: (lambda e: e.matmul(pO[:, 0:128], vtok(hd % 2)[:, ch * 128:(ch + 1) * 128], ptb[:, sl * 128:(sl + 1) * 128],
                                                           start=(sl == 0), stop=(sl == ntot - 1))))(sl, ch) for sl, ch in enumerate(chunks)]
                fns += [(lambda sl: (lambda e: e.matmul(pO[:, 128:256], onesb[:, :], ptb[:, sl * 128:(sl + 1) * 128],
                                                        start=(sl == 0), stop=(sl == ntot - 1))))(sl) for sl in range(ntot)]
                T.group("pe", fns, reads=[("vtok", hd % 2), "ptb", "ones"], writes=["pO"])
                T.op("dve", lambda e: e.reciprocal(out=rinv, in_=pO[:, 128:256]), reads=["pO"], writes=["rinv"])
                T.op("dve", lambda e: e.tensor_tensor(out=oT[:, hd, j * 128:(j + 1) * 128], in0=pO[:, 0:128], in1=rinv, op=ALU.mult),
                     reads=["pO", "rinv"], writes=[("oT", hd)])

            def push_attention(hd):
                for i in range(6):
                    def slot(i=i):
                        if 0 <= i - 2 < 4:
                            att_PV(hd, i - 2)
                        if 0 <= i - 1 < 4:
                            att_PT(hd, i - 1)
                        if i < 4:
                            att_S(hd, i)
                    att.append(slot)

            def rope_seg(pab, src_cols, dst, dname, si):
                sg = si % 2
                T.op("act", lambda e: e.activation(out=stg(sg), in_=pA[pab][:, :], func=AF.Copy), reads=[("pA", pab)], writes=[("stg", sg)])
                c0 = src_cols

                def fin():
                    T.group("pe", [lambda e: e.matmul(pR[:, 0:512], permf, stg(sg), start=True, stop=True)],
                            reads=[("stg", sg), "cstf"], writes=["pR"])
                    T.op("dve", lambda e: e.tensor_tensor(out=t1, in0=stg(sg), in1=ropeT[:, c0:c0 + 512], op=ALU.mult),
                         reads=[("stg", sg), "rope"], writes=["t1"])
                    T.op("dve", lambda e: e.tensor_tensor(out=t2, in0=pR[:, 0:512], in1=ropeT[:, KVL + c0:KVL + c0 + 512], op=ALU.mult),
                         reads=["pR", "rope"], writes=["t2"])
                    T.op("dve", lambda e: e.tensor_tensor(out=dst, in0=t1, in1=t2, op=ALU.add), reads=["t1", "t2"], writes=[dname])
                defer(fin)

            bias_dma(0, 0)
            for hd in range(NH):
                tick(first)
                s = next_wb(w_in[hd])
                b = next_pa()
                mm_group(pA[b][:, :], ("pA", b), s, lambda kc: hT[:, kc, 0:TT], TT, ALL_HT)
                rope_seg(b, 0, qT(hd % 2), ("qT", hd % 2), 0)
                s = next_wb(w_in[16 + hd])
                for seg in range(3):
                    tick(first)
                    b = next_pa()
                    n = 512 if seg < 2 else CTX
                    mm_group(pA[b][:, 0:n], ("pA", b), s, (lambda seg, n: (lambda kc: hT[:, kc, seg * 512:seg * 512 + n]))(seg, n), n, ALL_HT)
                    if seg < 2:
                        rope_seg(b, seg * 512, kT(hd % 2)[:, seg * 512:(seg + 1) * 512], ("kT", hd % 2), seg + 1)
                    else:
                        T.op("act", (lambda b, par: (lambda e: e.activation(out=kT(par)[:, KVL:KVT], in_=pA[b][:, 0:CTX], func=AF.Copy)))(b, hd % 2),
                             reads=[("pA", b)], writes=[("kT", hd % 2)])
                s = next_wb(w_in[32 + hd])
                for seg in range(3):
                    tick(first)
                    b = next_pa()
                    n = 512 if seg < 2 else CTX
                    mm_group(pA[b][:, 0:n], ("pA", b), s, (lambda seg, n: (lambda kc: hT[:, kc, seg * 512:seg * 512 + n]))(seg, n), n, ALL_HT)
                    T.op("act", (lambda b, seg, n: (lambda e: e.activation(out=vT[:, seg * 512:seg * 512 + n], in_=pA[b][:, 0:n], func=AF.Copy)))(b, seg, n),
                         reads=[("pA", b)], writes=["vT"])
                def vfin(hd=hd):
                    for half in range(2):
                        fns = [(lambda i, c: (lambda e: e.transpose(out=pRb[:, i * 128:(i + 1) * 128], in_=vT[:, c * 128:(c + 1) * 128],
                                                                    identity=identb[:, :])))(i, half * 5 + i) for i in range(5)]
                        T.group("pe", fns, reads=["vT", "ident"], writes=["pR"])
                        T.op("act", (lambda half, par: (lambda e: e.activation(out=vtok(par)[:, half * 640:(half + 1) * 640], in_=pRb[:, 0:640], func=AF.Copy)))(half, hd % 2),
                             reads=["pR"], writes=[("vtok", hd % 2)])
                    push_attention(hd)
                defer(vfin)
                if hd == 0:
                    chk("proj0")
                if hd == 1:
                    chk("att0")

            chk("att")
            flush_deferred()
            while att:
                tick(first)
            T.retire([("qT", 0), ("qT", 1), ("kT", 0), ("kT", 1), "vT", ("vtok", 0), ("vtok", 1), ("stg", 0), ("stg", 1), "t1", "t2",
                      ("bt", 0), ("bt", 1), "sc", "pb", "ptb", "rinv"], AR3_CONV + [("sq2", 0), ("sq2", 1)])
            T.retire(["pS"], [("pSg", 0), ("pSg", 1)])

            def gn_sq(ch, g, i, s):
                T.op("act", lambda e: e.activation(out=xo(s).bitcast(BF16)[:, 0:512], in_=oT[:, ch, :], func=AF.Square),
                     reads=[("oT", ch)], writes=[("xo", s)])
                defer(lambda: T.group("pe", [lambda e: e.matmul(pS[:, g * 512:(g + 1) * 512], onesb[:, :], xo(s).bitcast(BF16)[:, 0:512],
                                                                start=(i == 0), stop=(i == 15))], reads=[("xo", s), "ones"], writes=[("pSg", g)]))
            for cc in range(16):
                sU = next_wb(w_in[48 + cc])
                tick(first)
                bU = next_pa()
                mm_group(pA[bU][:, :], ("pA", bU), sU, lambda kc: hT[:, kc, 0:TT], TT, ALL_HT)
                mm_group(pR[:, 0:2], "pR", sU, lambda kc: hT[:, kc, 767:769], 2, ALL_HT)
                T.op("act", (lambda bU: (lambda e: e.activation(out=uS[:, 0:512], in_=pA[bU][:, :], func=AF.Copy)))(bU),
                     reads=[("pA", bU)], writes=["uS"])
                T.op("act", lambda e: e.activation(out=uS[:, 512:514], in_=pR[:, 0:2], func=AF.Copy), reads=["pR"], writes=["uS"])
                sC = next_wb(w_in[80 + cc])
                tick(first)
                bC = next_pa()
                mm_group(pA[bC][:, :], ("pA", bC), sC, lambda kc: hT[:, kc, 0:TT], TT, ALL_HT)
                mm_group(pR[:, 2:4], "pR", sC, lambda kc: hT[:, kc, 767:769], 2, ALL_HT)
                T.op("dve", (lambda bC: (lambda e: e.tensor_tensor(out=cup[:, 1:513], in0=pA[bC][:, :], in1=uS[:, 0:512], op=ALU.mult)))(bC),
                     reads=[("pA", bC), "uS"], writes=["cup"])
                T.op("dve", lambda e: e.scalar_tensor_tensor(out=cup[:, 0:1], in0=pR[:, 2:3], scalar=cmk[:, 0:1], in1=uS[:, 512:513],
                                                             op0=ALU.mult, op1=ALU.mult), reads=["pR", "uS", "cmk"], writes=["cup"])
                T.op("dve", lambda e: e.scalar_tensor_tensor(out=cup[:, 513:514], in0=pR[:, 3:4], scalar=cmk[:, 1:2], in1=uS[:, 513:514],
                                                             op0=ALU.mult, op1=ALU.mult), reads=["pR", "uS", "cmk"], writes=["cup"])
                T.op("dve", (lambda cc: (lambda e: e.tensor_scalar(out=tcv, in0=cup[:, 0:512], scalar1=cw[:, cc:cc + 1], scalar2=None, op0=ALU.mult)))(cc),
                     reads=["cup", "cw"], writes=["tcv"])
                T.op("dve", (lambda cc: (lambda e: e.scalar_tensor_tensor(out=tcv, in0=cup[:, 1:513], scalar=cw[:, 16 + cc:17 + cc], in1=tcv,
                                                                          op0=ALU.mult, op1=ALU.add)))(cc), reads=["cup", "cw", "tcv"], writes=["tcv"])
                T.op("dve", (lambda cc: (lambda e: e.scalar_tensor_tensor(out=tcv, in0=cup[:, 2:514], scalar=cw[:, 32 + cc:33 + cc], in1=tcv,
                                                                          op0=ALU.mult, op1=ALU.add)))(cc), reads=["cup", "cw", "tcv"], writes=["tcv"])
                sB = next_wb(w_in[64 + cc])
                tick(first)
                bB = next_pa()
                mm_group(pA[bB][:, :], ("pA", bB), sB, lambda kc: hT[:, kc, 0:TT], TT, ALL_HT)
                T.op("dve", (lambda bB, cc: (lambda e: e.tensor_tensor(out=oT[:, 16 + cc, :], in0=pA[bB][:, :], in1=tcv, op=ALU.mult)))(bB, cc),
                     reads=[("pA", bB), "tcv"], writes=[("oT", 16 + cc)])
                gn_sq(cc, 0, cc, 0)
                gn_sq(16 + cc, 1, cc, 1)

            chk("conv")
            if first:
                finish_mod()

            T.retire(ALL_HT, ALL_X1)
            T.retire(["rstdA", "rstdB"], [("rg", 0), ("rg", 1)])
            flush_deferred()
            for g in range(2):
                rms_from_psum(pS[:, g * 512:(g + 1) * 512], ("pSg", g), 512, rstd[:, g * 512:(g + 1) * 512], ("rg", g), 2048)
            for ch in range(KC):
                g = ch // 16
                T.op("dve", (lambda ch, g: (lambda e: e.scalar_tensor_tensor(out=oT[:, ch, :], in0=oT[:, ch, :], scalar=gn[:, KC + ch:KC + ch + 1],
                                                                             in1=rstd[:, g * 512:(g + 1) * 512], op0=ALU.mult, op1=ALU.mult)))(ch, g),
                     reads=[("oT", ch), ("rg", g), "gn"], writes=[("oT", ch)])

            def x1_sq(ch, pname, bank=None):
                bank = pS[:, 0:512] if bank is None else bank
                s = ch % 2
                T.op("act", lambda e: e.activation(out=sq2(s), in_=x1[:, ch, :], func=AF.Square), reads=[("x1", ch)], writes=[("sq2", s)])
                defer(lambda: T.group("pe", [lambda e: e.matmul(bank, onesb[:, :], sq2(s), start=(ch == 0), stop=(ch == KC - 1))],
                                      reads=[("sq2", s), "ones"], writes=[pname]))

            def x1_rstd_fin(pname, dname, bank=None):
                bank = pS[:, 0:512] if bank is None else bank
                flush_deferred()
                rms_from_psum(bank, pname, 512, rstd[:, 0:512], dname, D)

            for oc in range(KC):
                s = next_wb(w_out[oc])
                xsl = oc % 2
                T.dma("sp", (lambda oc, xsl: (lambda e: e.dma_start(out=xo(xsl), in_=xkv[ti, :, oc * KVT:oc * KVT + TT])))(oc, xsl),
                      writes=[("xo", xsl)])
                b = next_pa()
                mm_group(pA[b][:, :], ("pA", b), s, lambda kc: oT[:, kc, :], TT, ALL_OT)
                T.op("dve", (lambda oc, b, xsl: (lambda e: e.scalar_tensor_tensor(out=x1[:, oc, :], in0=pA[b][:, :], scalar=MOD(2, oc, 0),
                                                                                  in1=xo(xsl), op0=ALU.mult, op1=ALU.add)))(oc, b, xsl),
                     reads=[("pA", b), ("xo", xsl), ("mod", 2)], writes=[("x1", oc)])
                x1_sq(oc, ("pSg", 0))

            chk("wout")
            T.retire(ALL_OT, ALL_HN)
            T.retire(["rope", "uS", "cup", "tcv", ("xo", 0), ("xo", 1)], AR3_FFN)

            x1_rstd_fin(("pSg", 0), ("rg", 0))
            for ch in range(KC):
                T.op("dve", (lambda ch: (lambda e: e.tensor_tensor(out=tmpf, in0=x1[:, ch, :], in1=rstd[:, 0:512], op=ALU.mult)))(ch),
                     reads=[("x1", ch), ("rg", 0)], writes=["tmpf"])
                T.op("act", (lambda ch: (lambda e: e.activation(out=hn[:, ch, :], in_=tmpf, func=AF.Identity, bias=MOD(3, ch, 0),
                                                                scale=A2[:, ch:ch + 1])))(ch),
                     reads=["tmpf", "A2", ("mod", 3)], writes=[("hn", ch)])

            wdc = [0]
            for (q0, q1) in QUARTERS:
                nk = q1 - q0
                for f in range(q0, q1):
                    fl = f - q0
                    if bg2:
                        bg2.popleft()()
                    sg = next_wb(w_gu[2 * f])
                    bg_ = next_pa()
                    mm_group(pA[bg_][:, :], ("pA", bg_), sg, lambda kc: hn[:, kc, :], TT, ALL_HN)
                    gs = f % 2
                    T.op("act", (lambda bg_, gs: (lambda e: e.activation(out=gtt(gs), in_=pA[bg_][:, :], func=AF.Silu)))(bg_, gs),
                         reads=[("pA", bg_)], writes=[("gt", gs)])
                    su = next_wb(w_gu[2 * f + 1])
                    bu = next_pa()
                    mm_group(pA[bu][:, :], ("pA", bu), su, lambda kc: hn[:, kc, :], TT, ALL_HN)
                    T.op("dve", (lambda bu, gs, fl: (lambda e: e.tensor_tensor(out=actT[:, fl, :], in0=pA[bu][:, :], in1=gtt(gs), op=ALU.mult)))(bu, gs, fl),
                         reads=[("pA", bu), ("gt", gs)], writes=[("act", fl)])
                while bg2:
                    bg2.popleft()()
                flush_deferred()
                last_q = (q1 == FC)
                for oc in range(KC):
                    ws = st["wb"] % NWB
                    st["wb"] += 1
                    T.dma("pool", load_piece(w_dn[oc][:, q0 * 128:q1 * 128], nk * 128, wb(ws)[:, 0:nk * 128]), writes=[("wb", ws)])
                    b = next_pa()
                    fns = [(lambda kc, ws, b: (lambda e: e.matmul(pA[b][:, :], wb(ws)[:, kc * 128:(kc + 1) * 128], actT[:, kc, :],
                                                                  start=(kc == 0), stop=(kc == nk - 1))))(kc, ws, b) for kc in range(nk)]
                    T.group("pe", fns, reads=[("wb", ws)] + [("act", k) for k in range(nk)], writes=[("pA", b)])
                    flush_deferred()
                    T.op("dve", (lambda oc, b: (lambda e: e.scalar_tensor_tensor(out=x1[:, oc, :], in0=pA[b][:, :], scalar=MOD(5, oc, 0),
                                                                                 in1=x1[:, oc, :], op0=ALU.mult, op1=ALU.add)))(oc, b),
                         reads=[("pA", b), ("x1", oc), ("mod", 5)], writes=[("x1", oc)])
                    if last_q:
                        x1_sq(oc, "pR", pR[:, 0:512])
                        if ti + 1 < n_tiles:
                            if oc == 0:
                                begin_pass1(ti + 1)
                                early_pass1.add(ti + 1)
                            pass1_chunk(ti + 1, oc, True)

            chk("ffn")
            x1_rstd_fin("pR", ("rg", 0), pR[:, 0:512])
            for ch in range(KC):
                s = ch % 2
                T.op("dve", (lambda ch, s: (lambda e: e.scalar_tensor_tensor(out=ott(s), in0=x1[:, ch, :], scalar=gn[:, 3 * KC + ch:3 * KC + ch + 1],
                                                                             in1=rstd[:, 0:512], op0=ALU.mult, op1=ALU.mult)))(ch, s),
                     reads=[("x1", ch), ("rg", 0), "gn"], writes=[("ot", s)])
                out_toks.append(T.dma("sp", (lambda ch, s: (lambda e: e.dma_start(out=out[ti, :, ch * TT:(ch + 1) * TT], in_=ott(s))))(ch, s),
                                      reads=[("ot", s)]))

        try:
            for ti in range(n_tiles):
                emit_tile(ti)
        except _Stop:
            pass
        T.final_wait("sp", out_toks)

        with nc.Block() as block:
            @block.tensor
            def _(e):
                T.replay("pe", e)

            @block.scalar
            def _(e):
                T.replay("act", e)

            @block.vector
            def _(e):
                T.replay("dve", e)

            @block.gpsimd
            def _(e):
                T.replay("pool", e)

            @block.sync
            def _(e):
                T.replay("sp", e)
    return nc


def _fm(a):
    t = a.shape[0]
    return np.ascontiguousarray(a.T.reshape(KC, 128, t).transpose(1, 0, 2)).reshape(128, KC * t)


def _pieces(w):
    k, f = w.shape
    return np.ascontiguousarray(w.reshape(k // 128, 128, f // 128, 128).transpose(2, 1, 0, 3)).reshape(f // 128, 128, k)


def _ada_pieces(w):
    a = w.reshape(4, 8, 128, 48, 512).transpose(3, 0, 2, 1, 4)
    return np.ascontiguousarray(a).reshape(192, 128, 4096)


def _vecfm(v):
    n = v.shape[0] // 128
    return np.ascontiguousarray(v.reshape(n, 128).T)


def _tile_rows(t):
    rows = list(range(8 * t, 8 * t + 8)) + list(range(8 * t - 4, 8 * t)) + list(range(8 * t + 8, 8 * t + 12))
    return [r if 0 <= r < ROWS else -1 for r in rows]


def _bias_table(rpb, t):
    rows = np.array(_tile_rows(t))
    tab = np.full((NH, 4, 128, NSLOT * 128), NEG, np.float32)
    q = np.arange(128)
    kk = np.arange(128)
    for j in range(4):
        gq = 8 * t + 2 * j + q // 64
        qc = q % 64
        rs = np.clip(gq - 4, 0, ROWS - 8)
        cs = np.clip(qc - 8, 0, GW - 16)
        needed = set()
        for g in np.unique(gq):
            r0 = int(np.clip(g - 4, 0, ROWS - 8))
            needed |= set(range(r0, r0 + 8))
        have = set()
        for sl, ch in enumerate(BLK_CHUNKS[j]):
            gk = rows[2 * ch + kk // 64]
            kc = kk % 64
            have |= set(int(x) for x in gk if x >= 0)
            valid = ((gk[None, :] >= 0) & (gk[None, :] >= rs[:, None]) & (gk[None, :] < rs[:, None] + 8)
                     & (kc[None, :] >= cs[:, None]) & (kc[None, :] < cs[:, None] + 16))
            dr = np.clip(gk[None, :] - gq[:, None] + 7, 0, 14)
            dc = np.clip(kc[None, :] - qc[:, None] + 15, 0, 30)
            vals = rpb[:, dr, dc]
            tab[:, j, :, sl * 128:(sl + 1) * 128] = np.where(valid[None], vals, np.float32(NEG))
        assert needed <= have, (t, j, needed, have)
    return tab


def _rope_table(t):
    rows = np.array(_tile_rows(t))
    half = 32
    freqs = (np.float32(10000.0) ** (-np.arange(half, dtype=np.float32) / np.float32(half))).astype(np.float32)
    lr = np.arange(KVL) // 64
    rowpos = np.maximum(rows[lr], 0).astype(np.float32)
    colpos = (np.arange(KVL) % 64).astype(np.float32)
    p = np.arange(128)
    pos = np.where((p < 64)[:, None], rowpos[None, :], colpos[None, :]).astype(np.float32)
    ang = (pos * freqs[p % 32][:, None]).astype(np.float32)
    cos = np.cos(ang).astype(np.float32)
    sin = np.sin(ang).astype(np.float32)
    sgn = np.where((p % 64) < 32, -1.0, 1.0).astype(np.float32)[:, None]
    return np.concatenate([cos, sin * sgn], axis=1).astype(np.float32)


def shared_inputs(inp):
    l = 0
    w_gu = inp["w_gate_up"][l]
    pg = _pieces(np.ascontiguousarray(w_gu[:, :FFN]))
    pu = _pieces(np.ascontiguousarray(w_gu[:, FFN:]))
    gu = np.empty((2 * FC,) + pg.shape[1:], np.float32)
    gu[0::2] = pg
    gu[1::2] = pu
    ident = np.eye(128, dtype=np.float32)
    perm = ident[np.arange(128) ^ 32]
    cst = np.concatenate([ident, perm, np.ones((128, 128), np.float32)], axis=1)
    gains = np.concatenate([_vecfm(inp["norm1_g"][l]), _vecfm(inp["group_norm_g"][l]), _vecfm(inp["norm2_g"][l]),
                            _vecfm(inp["final_norm_g"])], axis=1)
    cw = inp["conv_w"][l]
    conv = np.ascontiguousarray(cw.reshape(3, 16, 128).transpose(2, 0, 1)).reshape(128, 48)
    return {
        "ada_w": _ada_pieces(inp["ada_w"][l]),
        "ada_b": _vecfm(inp["ada_b"][l]),
        "gains": np.ascontiguousarray(gains),
        "w_in": _pieces(inp["w_in"][l]),
        "conv_w": conv,
        "cst": np.ascontiguousarray(cst),
        "w_out": _pieces(inp["w_out"][l]),
        "w_gu": gu,
        "w_dn": _pieces(inp["w_down"][l]),
    }


def core_inputs(inp, c, n_tiles=N_TILES):
    b, half = c // 2, c % 2
    x = inp["x"][b]
    ctx = inp["ctx"][b]
    rpb = inp["rpb"][0]
    xk, ropes, cms, bts = [], [], [], []
    for ti in range(n_tiles):
        t = 2 * half + ti
        rows = _tile_rows(t)
        toks = np.zeros((KVT, D), np.float32)
        for lr, r in enumerate(rows):
            if r >= 0:
                toks[lr * 64:(lr + 1) * 64] = x[r * 64:(r + 1) * 64]
        toks[KVL:] = ctx
        xk.append(_fm(toks))
        ropes.append(_rope_table(t))
        cms.append(np.tile(np.array([[1.0 if t > 0 else 0.0, 1.0 if t < 3 else 0.0]], np.float32), (128, 1)))
        bts.append(_bias_table(rpb, t))
    cvec = np.stack([inp["c"][b], inp["c_ctx"]], axis=1)
    cvec = np.ascontiguousarray(cvec.reshape(KC, 128, 2).transpose(1, 0, 2)).reshape(128, KC * 2)
    return {
        "xkv": np.stack(xk),
        "cvec": cvec,
        "rope": np.stack(ropes),
        "cmask": np.stack(cms),
        "bias_t": np.concatenate(bts, axis=0).reshape(n_tiles * NH * 4, 128, NSLOT * 128),
    }


def gather_output(outs, n_tiles=N_TILES):
    full = np.zeros((4, S, D), np.float32)
    for c, o in enumerate(outs):
        b, half = c // 2, c % 2
        for ti in range(n_tiles):
            t = 2 * half + ti
            blk = o[ti].reshape(128, KC, TT).transpose(2, 1, 0).reshape(TT, D)
            full[b, t * TT:(t + 1) * TT] = blk
    return full


def kernel(**inputs):
    inp = {k: np.asarray(v) for k, v in inputs.items()}
    shared = shared_inputs(inp)
    nc = build_nc()
    in_maps = []
    for c in range(8):
        m = dict(shared)
        m.update(core_inputs(inp, c))
        in_maps.append(m)
    res = run_bass_kernel_spmd(nc, in_maps, core_ids=list(range(8)))
    return gather_output([r["out"] for r in res.results])
```

```python
from collections import deque
from contextlib import ExitStack

import numpy as np
import concourse.bass as bass
import concourse.mybir as mybir
from concourse.bass_utils import run_bass_kernel_spmd

F32 = mybir.dt.float32
BF16 = mybir.dt.bfloat16
AF = mybir.ActivationFunctionType
ALU = mybir.AluOpType
AX = mybir.AxisListType

D = 4096
KC = D // 128
S = 2048
GW = 64
ROWS = S // GW
CTX = 256
NH = 16
FFN = 11008
FC = FFN // 128
EPS = 1e-6
TT = 512
KVL = 1024
KVT = KVL + CTX
NEG = -1.0e30
SCALE = 128 ** -0.5
BLK_CHUNKS = [[0, 1, 2, 3, 4, 5], [0, 1, 2, 3, 5], [0, 1, 2, 3, 6], [0, 1, 2, 3, 6, 7]]
NSLOT = 6
N_TILES = 2
QUARTERS = [(0, 22), (22, 44), (44, 65), (65, 86)]
NWB = 4
NXS = 4
NOT = 6


class Trk:
    def __init__(self, sems, dma_sems):
        self.sem = sems
        self.cnt = {e: 0 for e in sems}
        self.plan = {e: [] for e in sems}
        self.lastw = {}
        self.reads = {}
        self.waited = {e: {} for e in sems}
        self.dma_sems = dma_sems
        self.dma_cnt = {q: [0] * len(v) for q, v in dma_sems.items()}
        self.dma_rr = {q: 0 for q in dma_sems}
        self.dma_out = {q: [None] * len(v) for q, v in dma_sems.items()}

    @staticmethod
    def _is_psum(r):
        if isinstance(r, tuple):
            return r[0] in ("pA", "pSg")
        return len(r) >= 2 and r[0] == "p" and r[1].isupper()

    def _excl(self, reads, writes):
        reads, writes = list(reads), list(writes)
        mv = [r for r in reads if self._is_psum(r)]
        return [r for r in reads if not self._is_psum(r)], writes + [r for r in mv if r not in writes]

    def _need(self, eng, deps):
        out = []
        for key, val in deps:
            if self.waited[eng].get(key, 0) >= val:
                continue
            self.waited[eng][key] = val
            out.append((key, val))
        return out

    def _deps(self, reads, writes):
        deps = []
        for r in reads:
            if r in self.lastw:
                deps.append(self.lastw[r])
        for w in writes:
            if w in self.lastw:
                deps.append(self.lastw[w])
            deps.extend(self.reads.get(w, []))
        return deps

    def _commit(self, tok, reads, writes):
        for r in reads:
            self.reads.setdefault(r, []).append(tok)
        for w in writes:
            self.lastw[w] = tok
            self.reads[w] = []

    def retire(self, old, new):
        deps = []
        for o in old:
            if o in self.lastw:
                deps.append(self.lastw.pop(o))
            deps.extend(self.reads.pop(o, []))
        for n in new:
            self.reads.setdefault(n, []).extend(deps)

    def op(self, eng, fn, reads=(), writes=()):
        reads, writes = self._excl(reads, writes)
        waits = self._need(eng, self._deps(reads, writes))
        self.cnt[eng] += 1
        tok = (("e", eng), self.cnt[eng])
        self.plan[eng].append((waits, fn, ("e", eng)))
        self._commit(tok, reads, writes)

    def group(self, eng, fns, reads=(), writes=()):
        reads, writes = self._excl(reads, writes)
        waits = self._need(eng, self._deps(reads, writes))
        n = len(fns)
        for i, fn in enumerate(fns):
            self.plan[eng].append((waits if i == 0 else [], fn, ("e", eng) if i == n - 1 else None))
        self.cnt[eng] += 1
        tok = (("e", eng), self.cnt[eng])
        self._commit(tok, reads, writes)

    def dma(self, q, fn, reads=(), writes=()):
        deps = self._deps(reads, writes)
        s = self.dma_rr[q]
        self.dma_rr[q] = (s + 1) % len(self.dma_sems[q])
        if self.dma_out[q][s] is not None:
            deps.append(self.dma_out[q][s])
        waits = self._need(q, deps)
        self.dma_cnt[q][s] += 16
        tok = (("d", q, s), self.dma_cnt[q][s])
        self.dma_out[q][s] = tok
        self.plan[q].append((waits, fn, ("d", q, s)))
        self._commit(tok, reads, writes)
        return tok

    def final_wait(self, eng, toks):
        waits = self._need(eng, list(toks))
        self.plan[eng].append((waits, None, None))

    def _semof(self, key):
        return self.sem[key[1]] if key[0] == "e" else self.dma_sems[key[1]][key[2]]

    def replay(self, eng, e):
        for waits, fn, sig in self.plan[eng]:
            for key, val in waits:
                e.wait_ge(self._semof(key), val)
            if fn is None:
                continue
            ins = fn(e)
            if sig is not None:
                ins.then_inc(self._semof(sig), 16 if sig[0] == "d" else 1)


def _segments(chunks):
    segs = []
    for s, ch in enumerate(chunks):
        k0, p0 = ch * 128, s * 128
        if segs and segs[-1][0] + segs[-1][2] == k0 and segs[-1][1] + segs[-1][2] == p0 and p0 % 512 != 0:
            segs[-1] = (segs[-1][0], segs[-1][1], segs[-1][2] + 128)
        else:
            segs.append((k0, p0, 128))
    p0 = len(chunks) * 128
    for c in range(2):
        k0 = KVL + c * 128
        pp = p0 + c * 128
        if segs[-1][0] + segs[-1][2] == k0 and segs[-1][1] + segs[-1][2] == pp and pp % 512 != 0:
            segs[-1] = (segs[-1][0], segs[-1][1], segs[-1][2] + 128)
        else:
            segs.append((k0, pp, 128))
    return segs


class _Stop(Exception):
    pass


def build_nc(n_tiles=N_TILES, stop=None):
    nc = bass.Bass("TRN2", target_bir_lowering=False)
    dt = nc.dram_tensor
    xkv = dt("xkv", [n_tiles, 128, KC * KVT], F32, kind="ExternalInput").ap()
    cvec = dt("cvec", [128, KC * 2], F32, kind="ExternalInput").ap()
    ada_w = dt("ada_w", [6 * KC, 128, KC * 128], F32, kind="ExternalInput").ap()
    ada_b = dt("ada_b", [128, 6 * KC], F32, kind="ExternalInput").ap()
    gains = dt("gains", [128, 4 * KC], F32, kind="ExternalInput").ap()
    w_in = dt("w_in", [96, 128, KC * 128], F32, kind="ExternalInput").ap()
    conv_w = dt("conv_w", [128, 48], F32, kind="ExternalInput").ap()
    rope = dt("rope", [n_tiles, 128, 2 * KVL], F32, kind="ExternalInput").ap()
    cmask = dt("cmask", [n_tiles, 128, 2], F32, kind="ExternalInput").ap()
    cst = dt("cst", [128, 3 * 128], F32, kind="ExternalInput").ap()
    bias_t = dt("bias_t", [n_tiles * NH * 4, 128, NSLOT * 128], F32, kind="ExternalInput").ap()
    w_out = dt("w_out", [KC, 128, KC * 128], F32, kind="ExternalInput").ap()
    w_gu = dt("w_gu", [2 * FC, 128, KC * 128], F32, kind="ExternalInput").ap()
    w_dn = dt("w_dn", [KC, 128, FC * 128], F32, kind="ExternalInput").ap()
    out = dt("out", [n_tiles, 128, KC * TT], F32, kind="ExternalOutput").ap()

    es = ExitStack()
    sb = lambda n, s, d: es.enter_context(nc.sbuf_tensor(n, s, d))
    ps = lambda n, s, d: es.enter_context(nc.psum_tensor(n, s, d))
    sm = lambda n: es.enter_context(nc.semaphore(n))
    with es:
        ENG = ["pe", "act", "dve", "pool", "sp"]
        sems = {e: sm("s_" + e) for e in ENG}
        dma_sems = {"sp": [sm(f"dsp{i}") for i in range(10)], "pool": [sm(f"dpl{i}") for i in range(10)]}
        T = Trk(sems, dma_sems)

        cstf = sb("cstf", [128, 384], F32)
        identb = sb("identb", [128, 128], BF16)
        onesb = sb("onesb", [128, 128], BF16)
        epsT = sb("epsT", [128, 1], F32)
        cv = sb("cv", [128, KC * 2], F32)
        cvb = sb("cvb", [128, KC * 2], BF16)
        adab = sb("adab", [128, 6 * KC], F32)
        gn = sb("gn", [128, 4 * KC], F32)
        cw = sb("cw", [128, 48], F32)
        cmk = sb("cmk", [128, 2], F32)
        mod = sb("mod", [128, 6 * KC * 2], F32)
        A1 = sb("A1", [128, KC * 2], F32)
        A2 = sb("A2", [128, KC], F32)
        negm = sb("negm", [128, 1], F32)
        m2 = sb("m2", [2, 512], F32)
        rstd = sb("rstd", [128, KVT], F32)
        wbt = sb("wbt", [128, NWB * KC * 128], BF16)
        ar1 = sb("ar1", [128, KC * KVT], BF16)
        ar2 = sb("ar2", [128, KC * TT], BF16)
        ar3 = sb("ar3", [128, 23552], BF16)
        permf = cstf[:, 128:256]

        def wb(s):
            return wbt[:, s * KC * 128:(s + 1) * KC * 128]

        hT = ar1[:, :].rearrange("p (k t) -> p k t", k=KC)
        x1 = ar1[:, 0:KC * TT * 2].bitcast(F32).rearrange("p (k t) -> p k t", k=KC)
        wdt = ar1[:, KC * TT * 2:KC * TT * 2 + 2 * 22 * 128]

        def wd(s):
            return wdt[:, s * 22 * 128:(s + 1) * 22 * 128]

        xs_ = ar2[:, 0:NXS * KVT * 2].bitcast(F32)
        sq_ = ar2[:, NXS * KVT * 2:NXS * KVT * 2 + NXS * KVT]
        xs = lambda s: xs_[:, s * KVT:(s + 1) * KVT]
        sq = lambda s: sq_[:, s * KVT:(s + 1) * KVT]
        oT = ar2[:, :].rearrange("p (k t) -> p k t", k=KC)
        hn = oT

        _o = [0]

        def carve(n_bf16):
            a = ar3[:, _o[0]:_o[0] + n_bf16]
            _o[0] += n_bf16
            return a

        ropeT = carve(2 * KVL * 2).bitcast(F32)
        qT2 = carve(2 * TT)
        kT2 = carve(2 * KVT)
        vT = carve(KVT)
        vtok2 = carve(2 * 10 * 128)
        stg_ = carve(2 * 512 * 2).bitcast(F32)
        t1 = carve(512 * 2).bitcast(F32)
        t2 = carve(512 * 2).bitcast(F32)
        bt_ = carve(2 * 768 * 2).bitcast(F32)
        sc = carve(1024 * 2).bitcast(F32)
        pb = carve(1024)
        ptb = carve(1024)
        rinv = carve(128 * 2).bitcast(F32)
        att_end = _o[0]
        qT = lambda par: qT2[:, par * TT:(par + 1) * TT]
        kT = lambda par: kT2[:, par * KVT:(par + 1) * KVT]
        vtok = lambda par: vtok2[:, par * 1280:(par + 1) * 1280]
        assert att_end <= 23552, att_end
        stg = lambda s: stg_[:, s * 512:(s + 1) * 512]
        bt = lambda s: bt_[:, s * 768:(s + 1) * 768]
        _o[0] = 2 * KVL * 2
        uS = carve(516 * 2).bitcast(F32)
        cup = carve(516 * 2).bitcast(F32)
        tcv = carve(512 * 2).bitcast(F32)
        xo_ = carve(2 * 512 * 2).bitcast(F32)
        xo = lambda s: xo_[:, s * 512:(s + 1) * 512]
        _o[0] = 0
        actT = carve(22 * 512).rearrange("p (k t) -> p k t", k=22)
        gt_ = carve(2 * 512 * 2).bitcast(F32)
        gtt = lambda s: gt_[:, s * 512:(s + 1) * 512]
        tmpf = carve(512 * 2).bitcast(F32)
        ot_ = carve(NOT * 512 * 2).bitcast(F32)
        ott = lambda s: ot_[:, s * 512:(s + 1) * 512]
        sq2_ = carve(2 * 512)
        sq2 = lambda s: sq2_[:, s * 512:(s + 1) * 512]
        assert _o[0] <= 23552, _o[0]

        pS = ps("pS", [128, 1024], F32)
        pT = ps("pT", [128, 512], F32)
        pTb = pT[:, :].bitcast(BF16)
        pM = ps("pM", [128, 512], F32)
        pA = [ps("pA0", [128, 512], F32), ps("pA1", [128, 512], F32)]
        pO = ps("pO", [128, 512], F32)
        pR = ps("pR", [128, 512], F32)
        pRb = pR[:, :].bitcast(BF16)

        st = {"wb": 0, "pa": 0, "xs": 0}

        def chk(name):
            if stop == name:
                raise _Stop()
        bg = deque()
        att = deque()

        def MOD(i, ch, r=0):
            k = 2 * (i * KC + ch) + r
            return mod[:, k:k + 1]

        def load_piece(src2d, n_el, dst):
            a = 2 if n_el <= 4096 else 4
            return lambda e: e.dma_start(out=dst.rearrange("p (a b) -> p a b", a=a),
                                         in_=src2d.rearrange("p (a b) -> p a b", a=a))

        def next_wb(src2d):
            s = st["wb"] % NWB
            st["wb"] += 1
            T.dma("pool", load_piece(src2d, KC * 128, wb(s)), writes=[("wb", s)])
            return s

        def next_pa():
            b = st["pa"] % 2
            st["pa"] += 1
            return b

        deferred = deque()

        def defer(fn):
            deferred.append(fn)

        def flush_deferred():
            while deferred:
                deferred.popleft()()

        def tick(allow_bg=True):
            if att:
                att.popleft()()
            if allow_bg and bg:
                bg.popleft()()

        def mm_group(psum_ap, pname, slot, rhs_fn, n, extra_reads):
            fns = [(lambda kc: (lambda e: e.matmul(psum_ap, wb(slot)[:, kc * 128:(kc + 1) * 128], rhs_fn(kc),
                                                   start=(kc == 0), stop=(kc == KC - 1))))(kc) for kc in range(KC)]
            T.group("pe", fns, reads=[("wb", slot)] + list(extra_reads), writes=[pname])
            flush_deferred()

        XS_REG = [("xs", i) for i in range(NXS)] + [("sq", i) for i in range(NXS)]
        ALL_HT = [("hT", k) for k in range(KC)]
        ALL_OT = [("oT", k) for k in range(KC)]
        ALL_HN = [("hn", k) for k in range(KC)]
        ALL_X1 = [("x1", k) for k in range(KC)]

        T.dma("sp", lambda e: e.dma_start(out=cstf[:, :], in_=cst), writes=["cstf"])
        T.dma("sp", lambda e: e.dma_start(out=cv[:, :], in_=cvec), writes=["cv"])
        T.dma("sp", lambda e: e.dma_start(out=adab[:, :], in_=ada_b), writes=["adab"])
        T.dma("sp", lambda e: e.dma_start(out=gn[:, :], in_=gains), writes=["gn"])
        T.dma("sp", lambda e: e.dma_start(out=cw[:, :], in_=conv_w), writes=["cw"])
        T.op("pool", lambda e: e.memset(epsT[:, :], EPS), writes=["eps"])
        T.op("dve", lambda e: e.tensor_copy(out=identb[:, :], in_=cstf[:, 0:128]), reads=["cstf"], writes=["ident"])
        T.op("dve", lambda e: e.tensor_copy(out=onesb[:, :], in_=cstf[:, 256:384]), reads=["cstf"], writes=["ones"])
        T.op("act", lambda e: e.activation(out=cvb[:, :], in_=cv[:, :], func=AF.Silu), reads=["cv"], writes=["cvb"])

        def ada_piece(pi):
            blk, g = pi // 4, pi % 4

            def f():
                if g == 0:
                    flush_deferred()
                s = next_wb(ada_w[pi])
                fns = [(lambda kci: (lambda e: e.matmul(pM[0:2, :], cvb[:, 2 * (g * 8 + kci):2 * (g * 8 + kci) + 2],
                                                        wb(s)[:, kci * 512:(kci + 1) * 512], start=(g == 0 and kci == 0),
                                                        stop=(g == 3 and kci == 7))))(kci) for kci in range(8)]
                T.group("pe", fns, reads=[("wb", s), "cvb"], writes=["pM"])
                if g == 3:
                    T.op("act", lambda e: e.activation(out=m2[:, :], in_=pM[0:2, :], func=AF.Copy), reads=["pM"], writes=["m2"])

                    def fin():
                        fns2 = [(lambda i: (lambda e: e.transpose(out=pO[:, 256 + 2 * i:258 + 2 * i], in_=m2[:, i * 128:(i + 1) * 128],
                                                                  identity=cstf[0:2, 0:2])))(i) for i in range(4)]
                        T.group("pe", fns2, reads=["m2", "cstf"], writes=["pO"])
                        c0 = blk * 4
                        T.op("dve", lambda e: e.tensor_tensor(
                            out=mod[:, 2 * c0:2 * c0 + 8].rearrange("p (a r) -> p a r", r=2),
                            in0=pO[:, 256:264].rearrange("p (a r) -> p a r", r=2),
                            in1=adab[:, c0:c0 + 4].unsqueeze(2).to_broadcast([128, 4, 2]), op=ALU.add),
                             reads=["pO", "adab"], writes=[("modb", blk)])
                    defer(fin)
            return f

        pre = deque(ada_piece(blk * 4 + g) for b_ in range(8) for blk in (b_, 8 + b_) for g in range(4))
        for pi in range(2 * KC, 6 * KC):
            bg.append(ada_piece(pi))
        bg2 = deque()
        A1v = A1[:, :].rearrange("p (k r) -> p k r", r=2)

        def A1_pair(b):
            a1 = A1[:, 8 * b:8 * b + 8]
            T.op("dve", lambda e: e.tensor_scalar(out=a1, in0=mod[:, 2 * KC + 8 * b:2 * KC + 8 * b + 8], scalar1=1.0, scalar2=None,
                                                  op0=ALU.add), reads=[("modb", 8 + b)], writes=[("A1", b)])
            a1v = a1.rearrange("p (k r) -> p k r", r=2)
            T.op("dve", lambda e: e.tensor_tensor(out=a1v, in0=a1v, in1=gn[:, 4 * b:4 * b + 4].unsqueeze(2).to_broadcast([128, 4, 2]),
                                                  op=ALU.mult), reads=[("A1", b), "gn"], writes=[("A1", b)])

        def finish_mod():
            while bg:
                bg.popleft()()
            flush_deferred()
            sc2v = mod[:, 2 * 4 * KC:2 * 5 * KC].rearrange("p (k r) -> p k r", r=2)[:, :, 0]
            T.op("dve", lambda e: e.tensor_scalar(out=A2[:, :], in0=sc2v, scalar1=1.0, scalar2=None, op0=ALU.add),
                 reads=[("modb", b_) for b_ in range(32, 40)], writes=["A2"])
            T.op("dve", lambda e: e.tensor_tensor(out=A2[:, :], in0=A2[:, :], in1=gn[:, 2 * KC:3 * KC], op=ALU.mult),
                 reads=["A2", "gn"], writes=["A2"])

        def rms_from_psum(psum_ap, pname, ncols, dst, dname, denom):
            T.op("act", lambda e: e.activation(out=dst, in_=psum_ap, func=AF.Sqrt, bias=epsT[:, 0:1],
                                               scale=1.0 / denom), reads=[pname, "eps"], writes=[dname])
            T.op("dve", lambda e: e.reciprocal(out=dst, in_=dst), reads=[dname], writes=[dname])

        out_toks = []

        early_pass1 = set()

        def begin_pass1(ti):
            T.retire(ALL_HN, XS_REG)
            T.retire([("pSg", 0), ("pSg", 1)], ["pS"])

        def pass1_chunk(ti, kc, early):
            s = st["xs"] % NXS
            st["xs"] += 1
            T.dma("sp", lambda e: e.dma_start(out=xs(s), in_=xkv[ti, :, kc * KVT:(kc + 1) * KVT]), writes=[("xs", s)])
            T.op("act", lambda e: e.activation(out=sq(s), in_=xs(s), func=AF.Square), reads=[("xs", s)], writes=[("sq", s)])
            fns = [
                lambda e: e.matmul(pS[:, 0:512], onesb[:, :], sq(s)[:, 0:512], start=(kc == 0), stop=(kc == KC - 1)),
                lambda e: e.matmul(pS[:, 512:1024], onesb[:, :], sq(s)[:, 512:1024], start=(kc == 0), stop=(kc == KC - 1)),
                lambda e: e.matmul(pT[:, 0:256], onesb[:, :], sq(s)[:, 1024:1280], start=(kc == 0), stop=(kc == KC - 1)),
            ]
            if early:
                defer(lambda: T.group("pe", fns, reads=[("sq", s), "ones"], writes=["pS", "pT"]))
            else:
                T.group("pe", fns, reads=[("sq", s), "ones"], writes=["pS", "pT"])

        def emit_tile(ti):
            first = ti == 0
            AR3_ATT = ["rope", ("qT", 0), ("qT", 1), ("kT", 0), ("kT", 1), "vT", ("vtok", 0), ("vtok", 1), ("stg", 0), ("stg", 1), "t1", "t2", ("bt", 0), ("bt", 1),
                       "sc", "pb", "ptb", "rinv"]
            AR3_CONV = ["uS", "cup", "tcv", ("xo", 0), ("xo", 1)]
            AR3_FFN = [("act", k) for k in range(22)] + [("gt", 0), ("gt", 1), "tmpf"] + [("ot", i) for i in range(NOT)] + [
                                                         ("sq2", 0), ("sq2", 1)]
            T.retire(ALL_X1, ALL_HT)
            if ti not in early_pass1:
                begin_pass1(ti)
            T.retire(AR3_FFN, AR3_ATT)
            T.retire([("rg", 0), ("rg", 1)], ["rstdA", "rstdB"])

            T.dma("sp", lambda e: e.dma_start(out=ropeT, in_=rope[ti]), writes=["rope"])
            T.dma("sp", lambda e: e.dma_start(out=cmk[:, :], in_=cmask[ti]), writes=["cmk"])

            chk("ada")
            if ti not in early_pass1:
                for kc in range(KC):
                    pass1_chunk(ti, kc, False)
                    if first and kc % 4 == 3:
                        pre.popleft()()
            flush_deferred()
            rms_from_psum(pS[:, 0:1024], "pS", 1024, rstd[:, 0:1024], "rstdA", D)
            rms_from_psum(pT[:, 0:256], "pT", 256, rstd[:, 1024:1280], "rstdB", D)
            def pass2_chunk(kc):
                s = st["xs"] % NXS
                st["xs"] += 1
                T.dma("sp", lambda e: e.dma_start(out=xs(s), in_=xkv[ti, :, kc * KVT:(kc + 1) * KVT]), writes=[("xs", s)])
                T.op("dve", lambda e: e.tensor_tensor(out=xs(s), in0=xs(s), in1=rstd[:, :], op=ALU.mult),
                     reads=[("xs", s), "rstdA", "rstdB"], writes=[("xs", s)])
                T.op("act", lambda e: e.activation(out=hT[:, kc, 0:KVL], in_=xs(s)[:, 0:KVL], func=AF.Identity,
                                                   bias=MOD(0, kc, 0), scale=A1[:, 2 * kc:2 * kc + 1]),
                     reads=[("xs", s), ("A1", kc // 4), ("modb", kc // 4)], writes=[("hT", kc)])
                T.op("act", lambda e: e.activation(out=hT[:, kc, KVL:KVT], in_=xs(s)[:, KVL:KVT], func=AF.Identity,
                                                   bias=MOD(0, kc, 1), scale=A1[:, 2 * kc + 1:2 * kc + 2]),
                     reads=[("xs", s), ("A1", kc // 4), ("modb", kc // 4)], writes=[("hT", kc)])

            for b_ in range(8):
                if first:
                    if b_ > 0:
                        for _ in range(8):
                            pre.popleft()()
                    flush_deferred()
                    A1_pair(b_)
                for kc in range(4 * b_, 4 * b_ + 4):
                    pass2_chunk(kc)
            chk("norm1")
            T.retire(XS_REG, ALL_OT)

            def bias_dma(hd, j):
                u = hd * 4 + j
                s = u % 2
                row = (ti * NH + hd) * 4 + j
                T.dma("sp", lambda e: e.dma_start(out=bt(s), in_=bias_t[row]), writes=[("bt", s)])

            def att_S(hd, j):
                chunks = BLK_CHUNKS[j]
                nlat = len(chunks) * 128
                ncols = nlat + CTX
                u = hd * 4 + j
                s = u % 2
                fns = [(lambda k0, p0, n: (lambda e: e.matmul(pS[:, p0:p0 + n], qT(hd % 2)[:, j * 128:(j + 1) * 128], kT(hd % 2)[:, k0:k0 + n],
                                                              start=True, stop=True)))(k0, p0, n) for (k0, p0, n) in _segments(chunks)]
                T.group("pe", fns, reads=[("qT", hd % 2), ("kT", hd % 2)], writes=["pS"])
                T.op("dve", lambda e: e.scalar_tensor_tensor(out=sc[:, 0:nlat], in0=pS[:, 0:nlat], scalar=SCALE, in1=bt(s)[:, 0:nlat],
                                                             op0=ALU.mult, op1=ALU.add), reads=["pS", ("bt", s)], writes=["sc"])
                T.op("dve", lambda e: e.tensor_scalar(out=sc[:, nlat:ncols], in0=pS[:, nlat:ncols], scalar1=SCALE, scalar2=None,
                                                      op0=ALU.mult), reads=["pS"], writes=["sc"])
                T.op("dve", lambda e: e.tensor_reduce(out=negm[:, 0:1], in_=sc[:, 0:ncols], axis=AX.X, op=ALU.max, negate=True),
                     reads=["sc"], writes=["negm"])
                T.op("act", lambda e: e.activation(out=pb[:, 0:ncols], in_=sc[:, 0:ncols], func=AF.Exp, bias=negm[:, 0:1], scale=1.0),
                     reads=["sc", "negm"], writes=["pb"])
                if j < 3:
                    bias_dma(hd, j + 1)
                elif hd + 1 < NH:
                    bias_dma(hd + 1, 0)

            def att_PT(hd, j):
                ntot = len(BLK_CHUNKS[j]) + 2
                fns = [(lambda sl: (lambda e: e.transpose(out=pTb[:, sl * 128:(sl + 1) * 128], in_=pb[:, sl * 128:(sl + 1) * 128],
                                                          identity=identb[:, :])))(sl) for sl in range(ntot)]
                T.group("pe", fns, reads=["pb", "ident"], writes=["pT"])
                T.op("act", lambda e: e.activation(out=ptb[:, 0:ntot * 128], in_=pTb[:, 0:ntot * 128], func=AF.Copy),
                     reads=["pT"], writes=["ptb"])

            def att_PV(hd, j):
                chunks = BLK_CHUNKS[j] + [8, 9]
                ntot = len(chunks)
                fns = [(lambda sl, ch: (lambda e: e.matmul(pO[:, 0:128], vtok(hd % 2)[:, ch * 128:(ch + 1) * 128], ptb[:, sl * 128:(sl + 1) * 128],
                                                           start=(sl == 0), stop=(sl == ntot - 1))))(sl, ch) for sl, ch in enumerate(chunks)]
                fns += [(lambda sl: (lambda e: e.matmul(pO[:, 128:256], onesb[:, :], ptb[:, sl * 128:(sl + 1) * 128],
                                                        start=(sl == 0), stop=(sl == ntot - 1))))(sl) for sl in range(ntot)]
                T.group("pe", fns, reads=[("vtok", hd % 2), "ptb", "ones"], writes=["pO"])
                T.op("dve", lambda e: e.reciprocal(out=rinv, in_=pO[:, 128:256]), reads=["pO"], writes=["rinv"])
                T.op("dve", lambda e: e.tensor_tensor(out=oT[:, hd, j * 128:(j + 1) * 128], in0=pO[:, 0:128], in1=rinv, op=ALU.mult),
                     reads=["pO", "rinv"], writes=[("oT", hd)])

            def push_attention(hd):
                for i in range(6):
                    def slot(i=i):
                        if 0 <= i - 2 < 4:
                            att_PV(hd, i - 2)
                        if 0 <= i - 1 < 4:
                            att_PT(hd, i - 1)
                        if i < 4:
                            att_S(hd, i)
                    att.append(slot)

            def rope_seg(pab, src_cols, dst, dname, si):
                sg = si % 2
                T.op("act", lambda e: e.activation(out=stg(sg), in_=pA[pab][:, :], func=AF.Copy), reads=[("pA", pab)], writes=[("stg", sg)])
                c0 = src_cols

                def fin():
                    T.group("pe", [lambda e: e.matmul(pR[:, 0:512], permf, stg(sg), start=True, stop=True)],
                            reads=[("stg", sg), "cstf"], writes=["pR"])
                    T.op("dve", lambda e: e.tensor_tensor(out=t1, in0=stg(sg), in1=ropeT[:, c0:c0 + 512], op=ALU.mult),
                         reads=[("stg", sg), "rope"], writes=["t1"])
                    T.op("dve", lambda e: e.tensor_tensor(out=t2, in0=pR[:, 0:512], in1=ropeT[:, KVL + c0:KVL + c0 + 512], op=ALU.mult),
                         reads=["pR", "rope"], writes=["t2"])
                    T.op("dve", lambda e: e.tensor_tensor(out=dst, in0=t1, in1=t2, op=ALU.add), reads=["t1", "t2"], writes=[dname])
                defer(fin)

            bias_dma(0, 0)
            for hd in range(NH):
                tick(first)
                s = next_wb(w_in[hd])
                b = next_pa()
                mm_group(pA[b][:, :], ("pA", b), s, lambda kc: hT[:, kc, 0:TT], TT, ALL_HT)
                rope_seg(b, 0, qT(hd % 2), ("qT", hd % 2), 0)
                s = next_wb(w_in[16 + hd])
                for seg in range(3):
                    tick(first)
                    b = next_pa()
                    n = 512 if seg < 2 else CTX
                    mm_group(pA[b][:, 0:n], ("pA", b), s, (lambda seg, n: (lambda kc: hT[:, kc, seg * 512:seg * 512 + n]))(seg, n), n, ALL_HT)
                    if seg < 2:
                        rope_seg(b, seg * 512, kT(hd % 2)[:, seg * 512:(seg + 1) * 512], ("kT", hd % 2), seg + 1)
                    else:
                        T.op("act", (lambda b, par: (lambda e: e.activation(out=kT(par)[:, KVL:KVT], in_=pA[b][:, 0:CTX], func=AF.Copy)))(b, hd % 2),
                             reads=[("pA", b)], writes=[("kT", hd % 2)])
                s = next_wb(w_in[32 + hd])
                for seg in range(3):
                    tick(first)
                    b = next_pa()
                    n = 512 if seg < 2 else CTX
                    mm_group(pA[b][:, 0:n], ("pA", b), s, (lambda seg, n: (lambda kc: hT[:, kc, seg * 512:seg * 512 + n]))(seg, n), n, ALL_HT)
                    T.op("act", (lambda b, seg, n: (lambda e: e.activation(out=vT[:, seg * 512:seg * 512 + n], in_=pA[b][:, 0:n], func=AF.Copy)))(b, seg, n),
                         reads=[("pA", b)], writes=["vT"])
                def vfin(hd=hd):
                    for half in range(2):
                        fns = [(lambda i, c: (lambda e: e.transpose(out=pRb[:, i * 128:(i + 1) * 128], in_=vT[:, c * 128:(c + 1) * 128],
                                                                    identity=identb[:, :])))(i, half * 5 + i) for i in range(5)]
                        T.group("pe", fns, reads=["vT", "ident"], writes=["pR"])
                        T.op("act", (lambda half, par: (lambda e: e.activation(out=vtok(par)[:, half * 640:(half + 1) * 640], in_=pRb[:, 0:640], func=AF.Copy)))(half, hd % 2),
                             reads=["pR"], writes=[("vtok", hd % 2)])
                    push_attention(hd)
                defer(vfin)
                if hd == 0:
                    chk("proj0")
                if hd == 1:
                    chk("att0")

            chk("att")
            flush_deferred()
            while att:
                tick(first)
            T.retire([("qT", 0), ("qT", 1), ("kT", 0), ("kT", 1), "vT", ("vtok", 0), ("vtok", 1), ("stg", 0), ("stg", 1), "t1", "t2",
                      ("bt", 0), ("bt", 1), "sc", "pb", "ptb", "rinv"], AR3_CONV + [("sq2", 0), ("sq2", 1)])
            T.retire(["pS"], [("pSg", 0), ("pSg", 1)])

            def gn_sq(ch, g, i, s):
                T.op("act", lambda e: e.activation(out=xo(s).bitcast(BF16)[:, 0:512], in_=oT[:, ch, :], func=AF.Square),
                     reads=[("oT", ch)], writes=[("xo", s)])
                defer(lambda: T.group("pe", [lambda e: e.matmul(pS[:, g * 512:(g + 1) * 512], onesb[:, :], xo(s).bitcast(BF16)[:, 0:512],
                                                                start=(i == 0), stop=(i == 15))], reads=[("xo", s), "ones"], writes=[("pSg", g)]))
            for cc in range(16):
                sU = next_wb(w_in[48 + cc])
                tick(first)
                bU = next_pa()
                mm_group(pA[bU][:, :], ("pA", bU), sU, lambda kc: hT[:, kc, 0:TT], TT, ALL_HT)
                mm_group(pR[:, 0:2], "pR", sU, lambda kc: hT[:, kc, 767:769], 2, ALL_HT)
                T.op("act", (lambda bU: (lambda e: e.activation(out=uS[:, 0:512], in_=pA[bU][:, :], func=AF.Copy)))(bU),
                     reads=[("pA", bU)], writes=["uS"])
                T.op("act", lambda e: e.activation(out=uS[:, 512:514], in_=pR[:, 0:2], func=AF.Copy), reads=["pR"], writes=["uS"])
                sC = next_wb(w_in[80 + cc])
                tick(first)
                bC = next_pa()
                mm_group(pA[bC][:, :], ("pA", bC), sC, lambda kc: hT[:, kc, 0:TT], TT, ALL_HT)
                mm_group(pR[:, 2:4], "pR", sC, lambda kc: hT[:, kc, 767:769], 2, ALL_HT)
                T.op("dve", (lambda bC: (lambda e: e.tensor_tensor(out=cup[:, 1:513], in0=pA[bC][:, :], in1=uS[:, 0:512], op=ALU.mult)))(bC),
                     reads=[("pA", bC), "uS"], writes=["cup"])
                T.op("dve", lambda e: e.scalar_tensor_tensor(out=cup[:, 0:1], in0=pR[:, 2:3], scalar=cmk[:, 0:1], in1=uS[:, 512:513],
                                                             op0=ALU.mult, op1=ALU.mult), reads=["pR", "uS", "cmk"], writes=["cup"])
                T.op("dve", lambda e: e.scalar_tensor_tensor(out=cup[:, 513:514], in0=pR[:, 3:4], scalar=cmk[:, 1:2], in1=uS[:, 513:514],
                                                             op0=ALU.mult, op1=ALU.mult), reads=["pR", "uS", "cmk"], writes=["cup"])
                T.op("dve", (lambda cc: (lambda e: e.tensor_scalar(out=tcv, in0=cup[:, 0:512], scalar1=cw[:, cc:cc + 1], scalar2=None, op0=ALU.mult)))(cc),
                     reads=["cup", "cw"], writes=["tcv"])
                T.op("dve", (lambda cc: (lambda e: e.scalar_tensor_tensor(out=tcv, in0=cup[:, 1:513], scalar=cw[:, 16 + cc:17 + cc], in1=tcv,
                                                                          op0=ALU.mult, op1=ALU.add)))(cc), reads=["cup", "cw", "tcv"], writes=["tcv"])
                T.op("dve", (lambda cc: (lambda e: e.scalar_tensor_tensor(out=tcv, in0=cup[:, 2:514], scalar=cw[:, 32 + cc:33 + cc], in1=tcv,
                                                                          op0=ALU.mult, op1=ALU.add)))(cc), reads=["cup", "cw", "tcv"], writes=["tcv"])
                sB = next_wb(w_in[64 + cc])
                tick(first)
                bB = next_pa()
                mm_group(pA[bB][:, :], ("pA", bB), sB, lambda kc: hT[:, kc, 0:TT], TT, ALL_HT)
                T.op("dve", (lambda bB, cc: (lambda e: e.tensor_tensor(out=oT[:, 16 + cc, :], in0=pA[bB][:, :], in1=tcv, op=ALU.mult)))(bB, cc),
                     reads=[("pA", bB), "tcv"], writes=[("oT", 16 + cc)])
                gn_sq(cc, 0, cc, 0)
                gn_sq(16 + cc, 1, cc, 1)

            chk("conv")
            if first:
                finish_mod()

            T.retire(ALL_HT, ALL_X1)
            T.retire(["rstdA", "rstdB"], [("rg", 0), ("rg", 1)])
            flush_deferred()
            for g in range(2):
                rms_from_psum(pS[:, g * 512:(g + 1) * 512], ("pSg", g), 512, rstd[:, g * 512:(g + 1) * 512], ("rg", g), 2048)
            for ch in range(KC):
                g = ch // 16
                T.op("dve", (lambda ch, g: (lambda e: e.scalar_tensor_tensor(out=oT[:, ch, :], in0=oT[:, ch, :], scalar=gn[:, KC + ch:KC + ch + 1],
                                                                             in1=rstd[:, g * 512:(g + 1) * 512], op0=ALU.mult, op1=ALU.mult)))(ch, g),
                     reads=[("oT", ch), ("rg", g), "gn"], writes=[("oT", ch)])

            def x1_sq(ch, pname, bank=None):
                bank = pS[:, 0:512] if bank is None else bank
                s = ch % 2
                T.op("act", lambda e: e.activation(out=sq2(s), in_=x1[:, ch, :], func=AF.Square), reads=[("x1", ch)], writes=[("sq2", s)])
                defer(lambda: T.group("pe", [lambda e: e.matmul(bank, onesb[:, :], sq2(s), start=(ch == 0), stop=(ch == KC - 1))],
                                      reads=[("sq2", s), "ones"], writes=[pname]))

            def x1_rstd_fin(pname, dname, bank=None):
                bank = pS[:, 0:512] if bank is None else bank
                flush_deferred()
                rms_from_psum(bank, pname, 512, rstd[:, 0:512], dname, D)

            for oc in range(KC):
                s = next_wb(w_out[oc])
                xsl = oc % 2
                T.dma("sp", (lambda oc, xsl: (lambda e: e.dma_start(out=xo(xsl), in_=xkv[ti, :, oc * KVT:oc * KVT + TT])))(oc, xsl),
                      writes=[("xo", xsl)])
                b = next_pa()
                mm_group(pA[b][:, :], ("pA", b), s, lambda kc: oT[:, kc, :], TT, ALL_OT)
                T.op("dve", (lambda oc, b, xsl: (lambda e: e.scalar_tensor_tensor(out=x1[:, oc, :], in0=pA[b][:, :], scalar=MOD(2, oc, 0),
                                                                                  in1=xo(xsl), op0=ALU.mult, op1=ALU.add)))(oc, b, xsl),
                     reads=[("pA", b), ("xo", xsl), ("modb", (2 * KC + oc) // 4)], writes=[("x1", oc)])
                x1_sq(oc, ("pSg", 0))

            chk("wout")
            T.retire(ALL_OT, ALL_HN)
            T.retire(["rope", "uS", "cup", "tcv", ("xo", 0), ("xo", 1)], AR3_FFN)

            x1_rstd_fin(("pSg", 0), ("rg", 0))
            for ch in range(KC):
                T.op("dve", (lambda ch: (lambda e: e.tensor_tensor(out=tmpf, in0=x1[:, ch, :], in1=rstd[:, 0:512], op=ALU.mult)))(ch),
                     reads=[("x1", ch), ("rg", 0)], writes=["tmpf"])
                T.op("act", (lambda ch: (lambda e: e.activation(out=hn[:, ch, :], in_=tmpf, func=AF.Identity, bias=MOD(3, ch, 0),
                                                                scale=A2[:, ch:ch + 1])))(ch),
                     reads=["tmpf", "A2", ("modb", (3 * KC + ch) // 4)], writes=[("hn", ch)])

            wdc = [0]
            for (q0, q1) in QUARTERS:
                nk = q1 - q0
                for f in range(q0, q1):
                    fl = f - q0
                    if bg2:
                        bg2.popleft()()
                    sg = next_wb(w_gu[2 * f])
                    bg_ = next_pa()
                    mm_group(pA[bg_][:, :], ("pA", bg_), sg, lambda kc: hn[:, kc, :], TT, ALL_HN)
                    gs = f % 2
                    T.op("act", (lambda bg_, gs: (lambda e: e.activation(out=gtt(gs), in_=pA[bg_][:, :], func=AF.Silu)))(bg_, gs),
                         reads=[("pA", bg_)], writes=[("gt", gs)])
                    su = next_wb(w_gu[2 * f + 1])
                    bu = next_pa()
                    mm_group(pA[bu][:, :], ("pA", bu), su, lambda kc: hn[:, kc, :], TT, ALL_HN)
                    T.op("dve", (lambda bu, gs, fl: (lambda e: e.tensor_tensor(out=actT[:, fl, :], in0=pA[bu][:, :], in1=gtt(gs), op=ALU.mult)))(bu, gs, fl),
                         reads=[("pA", bu), ("gt", gs)], writes=[("act", fl)])
                while bg2:
                    bg2.popleft()()
                flush_deferred()
                last_q = (q1 == FC)
                for oc in range(KC):
                    ws = st["wb"] % NWB
                    st["wb"] += 1
                    T.dma("pool", load_piece(w_dn[oc][:, q0 * 128:q1 * 128], nk * 128, wb(ws)[:, 0:nk * 128]), writes=[("wb", ws)])
                    b = next_pa()
                    fns = [(lambda kc, ws, b: (lambda e: e.matmul(pA[b][:, :], wb(ws)[:, kc * 128:(kc + 1) * 128], actT[:, kc, :],
                                                                  start=(kc == 0), stop=(kc == nk - 1))))(kc, ws, b) for kc in range(nk)]
                    T.group("pe", fns, reads=[("wb", ws)] + [("act", k) for k in range(nk)], writes=[("pA", b)])
                    flush_deferred()
                    T.op("dve", (lambda oc, b: (lambda e: e.scalar_tensor_tensor(out=x1[:, oc, :], in0=pA[b][:, :], scalar=MOD(5, oc, 0),
                                                                                 in1=x1[:, oc, :], op0=ALU.mult, op1=ALU.add)))(oc, b),
                         reads=[("pA", b), ("x1", oc), ("modb", (5 * KC + oc) // 4)], writes=[("x1", oc)])
                    if last_q:
                        x1_sq(oc, "pR", pR[:, 0:512])
                        if ti + 1 < n_tiles:
                            if oc == 0:
                                begin_pass1(ti + 1)
                                early_pass1.add(ti + 1)
                            pass1_chunk(ti + 1, oc, True)

            chk("ffn")
            x1_rstd_fin("pR", ("rg", 0), pR[:, 0:512])
            for ch in range(KC):
                s = ch % NOT
                T.op("dve", (lambda ch, s: (lambda e: e.scalar_tensor_tensor(out=ott(s), in0=x1[:, ch, :], scalar=gn[:, 3 * KC + ch:3 * KC + ch + 1],
                                                                             in1=rstd[:, 0:512], op0=ALU.mult, op1=ALU.mult)))(ch, s),
                     reads=[("x1", ch), ("rg", 0), "gn"], writes=[("ot", s)])
                out_toks.append(T.dma("sp", (lambda ch, s: (lambda e: e.dma_start(out=out[ti, :, ch * TT:(ch + 1) * TT], in_=ott(s))))(ch, s),
                                      reads=[("ot", s)]))

        try:
            for ti in range(n_tiles):
                emit_tile(ti)
        except _Stop:
            pass
        T.final_wait("sp", out_toks)

        with nc.Block() as block:
            @block.tensor
            def _(e):
                T.replay("pe", e)

            @block.scalar
            def _(e):
                T.replay("act", e)

            @block.vector
            def _(e):
                T.replay("dve", e)

            @block.gpsimd
            def _(e):
                T.replay("pool", e)

            @block.sync
            def _(e):
                T.replay("sp", e)
    return nc


def _fm(a):
    t = a.shape[0]
    return np.ascontiguousarray(a.T.reshape(KC, 128, t).transpose(1, 0, 2)).reshape(128, KC * t)


def _pieces(w):
    k, f = w.shape
    return np.ascontiguousarray(w.reshape(k // 128, 128, f // 128, 128).transpose(2, 1, 0, 3)).reshape(f // 128, 128, k)


def _ada_pieces(w):
    a = w.reshape(4, 8, 128, 48, 512).transpose(3, 0, 2, 1, 4)
    return np.ascontiguousarray(a).reshape(192, 128, 4096)


def _vecfm(v):
    n = v.shape[0] // 128
    return np.ascontiguousarray(v.reshape(n, 128).T)


def _tile_rows(t):
    rows = list(range(8 * t, 8 * t + 8)) + list(range(8 * t - 4, 8 * t)) + list(range(8 * t + 8, 8 * t + 12))
    return [r if 0 <= r < ROWS else -1 for r in rows]


def _bias_table(rpb, t):
    rows = np.array(_tile_rows(t))
    tab = np.full((NH, 4, 128, NSLOT * 128), NEG, np.float32)
    q = np.arange(128)
    kk = np.arange(128)
    for j in range(4):
        gq = 8 * t + 2 * j + q // 64
        qc = q % 64
        rs = np.clip(gq - 4, 0, ROWS - 8)
        cs = np.clip(qc - 8, 0, GW - 16)
        needed = set()
        for g in np.unique(gq):
            r0 = int(np.clip(g - 4, 0, ROWS - 8))
            needed |= set(range(r0, r0 + 8))
        have = set()
        for sl, ch in enumerate(BLK_CHUNKS[j]):
            gk = rows[2 * ch + kk // 64]
            kc = kk % 64
            have |= set(int(x) for x in gk if x >= 0)
            valid = ((gk[None, :] >= 0) & (gk[None, :] >= rs[:, None]) & (gk[None, :] < rs[:, None] + 8)
                     & (kc[None, :] >= cs[:, None]) & (kc[None, :] < cs[:, None] + 16))
            dr = np.clip(gk[None, :] - gq[:, None] + 7, 0, 14)
            dc = np.clip(kc[None, :] - qc[:, None] + 15, 0, 30)
            vals = rpb[:, dr, dc]
            tab[:, j, :, sl * 128:(sl + 1) * 128] = np.where(valid[None], vals, np.float32(NEG))
        assert needed <= have, (t, j, needed, have)
    return tab


def _rope_table(t):
    rows = np.array(_tile_rows(t))
    half = 32
    freqs = (np.float32(10000.0) ** (-np.arange(half, dtype=np.float32) / np.float32(half))).astype(np.float32)
    lr = np.arange(KVL) // 64
    rowpos = np.maximum(rows[lr], 0).astype(np.float32)
    colpos = (np.arange(KVL) % 64).astype(np.float32)
    p = np.arange(128)
    pos = np.where((p < 64)[:, None], rowpos[None, :], colpos[None, :]).astype(np.float32)
    ang = (pos * freqs[p % 32][:, None]).astype(np.float32)
    cos = np.cos(ang).astype(np.float32)
    sin = np.sin(ang).astype(np.float32)
    sgn = np.where((p % 64) < 32, -1.0, 1.0).astype(np.float32)[:, None]
    return np.concatenate([cos, sin * sgn], axis=1).astype(np.float32)


def shared_inputs(inp):
    l = 0
    w_gu = inp["w_gate_up"][l]
    pg = _pieces(np.ascontiguousarray(w_gu[:, :FFN]))
    pu = _pieces(np.ascontiguousarray(w_gu[:, FFN:]))
    gu = np.empty((2 * FC,) + pg.shape[1:], np.float32)
    gu[0::2] = pg
    gu[1::2] = pu
    ident = np.eye(128, dtype=np.float32)
    perm = ident[np.arange(128) ^ 32]
    cst = np.concatenate([ident, perm, np.ones((128, 128), np.float32)], axis=1)
    gains = np.concatenate([_vecfm(inp["norm1_g"][l]), _vecfm(inp["group_norm_g"][l]), _vecfm(inp["norm2_g"][l]),
                            _vecfm(inp["final_norm_g"])], axis=1)
    cw = inp["conv_w"][l]
    conv = np.ascontiguousarray(cw.reshape(3, 16, 128).transpose(2, 0, 1)).reshape(128, 48)
    return {
        "ada_w": _ada_pieces(inp["ada_w"][l]),
        "ada_b": _vecfm(inp["ada_b"][l]),
        "gains": np.ascontiguousarray(gains),
        "w_in": _pieces(inp["w_in"][l]),
        "conv_w": conv,
        "cst": np.ascontiguousarray(cst),
        "w_out": _pieces(inp["w_out"][l]),
        "w_gu": gu,
        "w_dn": _pieces(inp["w_down"][l]),
    }


def core_inputs(inp, c, n_tiles=N_TILES):
    b, half = c // 2, c % 2
    x = inp["x"][b]
    ctx = inp["ctx"][b]
    rpb = inp["rpb"][0]
    xk, ropes, cms, bts = [], [], [], []
    for ti in range(n_tiles):
        t = 2 * half + ti
        rows = _tile_rows(t)
        toks = np.zeros((KVT, D), np.float32)
        for lr, r in enumerate(rows):
            if r >= 0:
                toks[lr * 64:(lr + 1) * 64] = x[r * 64:(r + 1) * 64]
        toks[KVL:] = ctx
        xk.append(_fm(toks))
        ropes.append(_rope_table(t))
        cms.append(np.tile(np.array([[1.0 if t > 0 else 0.0, 1.0 if t < 3 else 0.0]], np.float32), (128, 1)))
        bts.append(_bias_table(rpb, t))
    cvec = np.stack([inp["c"][b], inp["c_ctx"]], axis=1)
    cvec = np.ascontiguousarray(cvec.reshape(KC, 128, 2).transpose(1, 0, 2)).reshape(128, KC * 2)
    return {
        "xkv": np.stack(xk),
        "cvec": cvec,
        "rope": np.stack(ropes),
        "cmask": np.stack(cms),
        "bias_t": np.concatenate(bts, axis=0).reshape(n_tiles * NH * 4, 128, NSLOT * 128),
    }


def gather_output(outs, n_tiles=N_TILES):
    full = np.zeros((4, S, D), np.float32)
    for c, o in enumerate(outs):
        b, half = c // 2, c % 2
        for ti in range(n_tiles):
            t = 2 * half + ti
            blk = o[ti].reshape(128, KC, TT).transpose(2, 1, 0).reshape(TT, D)
            full[b, t * TT:(t + 1) * TT] = blk
    return full


def kernel(**inputs):
    inp = {k: np.asarray(v) for k, v in inputs.items()}
    shared = shared_inputs(inp)
    nc = build_nc()
    in_maps = []
    for c in range(8):
        m = dict(shared)
        m.update(core_inputs(inp, c))
        in_maps.append(m)
    res = run_bass_kernel_spmd(nc, in_maps, core_ids=list(range(8)))
    return gather_output([r["out"] for r in res.results])
```
